# Optimizing a Trainium2 kernel written in Bass

```python
import jax, jax.numpy as jnp
from jax import lax
import numpy as np

D_MODEL = 2048
BATCH = 4
SEQ = 8192
DEPTH = 2
DEC_BATCH = 8
DEC_SEQ = 16
PAST_LEN = 2048

CHUNK = 64
EPS = 1e-6
ATTN_HEADS = 16
ATTN_KV_HEADS = 2
ATTN_GROUP = ATTN_HEADS // ATTN_KV_HEADS
HEAD_DIM = 64
WINDOW = 128
WINDOW_CHUNKS = WINDOW // CHUNK
ATTN_Q_DIM = ATTN_HEADS * HEAD_DIM
ATTN_KV_DIM = ATTN_KV_HEADS * HEAD_DIM
SSD_HEADS = 16
SSD_HEAD_DIM = 64
SSD_INNER = SSD_HEADS * SSD_HEAD_DIM
SSD_STATE = 128
SSD_GROUPS = 2
SSD_HPG = SSD_HEADS // SSD_GROUPS
SSD_CONV = 4
SSD_CHUNK = 64
SSD_CONV_DIM = SSD_INNER + 2 * SSD_GROUPS * SSD_STATE
IN0_DIM = ATTN_Q_DIM + 2 * ATTN_KV_DIM + SSD_INNER + SSD_CONV_DIM + SSD_HEADS
MIX0_DIM = ATTN_Q_DIM + SSD_INNER
SCONV_WIDTH = 3
D_FF = 5632
N_MOD = 9
N_EVEN = (DEPTH + 1) // 2
N_ODD = DEPTH // 2

kernel_name = "hybrid_streaming_swa_ssd_shortconv_step"


def rmsnorm(x, g):
    x32 = x.astype(jnp.float32)
    y = x32 * lax.rsqrt(jnp.mean(x32 * x32, axis=-1, keepdims=True) + EPS)
    return (y * g.astype(jnp.float32)).astype(x.dtype)


def swiglu(h, wg, wu, wd):
    return (jax.nn.silu(h @ wg) * (h @ wu)) @ wd


def causal_dwconv(x, prev, w):
    k = w.shape[0]
    length = x.shape[1]
    xp = jnp.concatenate([prev.astype(x.dtype), x], axis=1)
    y = xp[:, 0:length] * w[0]
    for i in range(1, k):
        y = y + xp[:, i:i + length] * w[i]
    return y, xp[:, -(k - 1):]


def alibi_slopes():
    return jnp.asarray(2.0 ** (-8.0 * np.arange(1, ATTN_HEADS + 1) / ATTN_HEADS), jnp.float32)


def sink_attention(q, k, v, dist, valid, sinks):
    slopes = alibi_slopes().reshape(ATTN_KV_HEADS, ATTN_GROUP)
    s = jnp.einsum('bnqhgd,bnshd->bnhgqs', q, k).astype(jnp.float32) * (HEAD_DIM ** -0.5)
    s = s - slopes[:, :, None, None] * dist[None, :, None, None]
    s = jnp.where(valid[None, :, None, None], s, -jnp.inf)
    sink = jnp.broadcast_to(sinks.astype(jnp.float32).reshape(ATTN_KV_HEADS, ATTN_GROUP)[:, :, None, None],
                            s.shape[:-1] + (1,))
    p = jax.nn.softmax(jnp.concatenate([s, sink], axis=-1), axis=-1)[..., :-1]
    return jnp.einsum('bnhgqs,bnshd->bnqhgd', p.astype(v.dtype), v)


def swa_prompt(q, k, v, sinks):
    b, length = q.shape[:2]
    nc = length // CHUNK
    qc = q.reshape(b, nc, CHUNK, ATTN_KV_HEADS, ATTN_GROUP, HEAD_DIM)

    def band(t):
        tc = t.reshape(b, nc, CHUNK, ATTN_KV_HEADS, HEAD_DIM)
        tp = jnp.pad(tc, ((0, 0), (WINDOW_CHUNKS, 0), (0, 0), (0, 0), (0, 0)))
        return jnp.concatenate([tp[:, i:i + nc] for i in range(WINDOW_CHUNKS + 1)], axis=2)

    kb, vb = band(k), band(v)
    qi = jnp.arange(CHUNK)
    kj = jnp.arange((WINDOW_CHUNKS + 1) * CHUNK)
    dist = jnp.abs(qi[:, None] + WINDOW_CHUNKS * CHUNK - kj[None, :]).astype(jnp.float32)[None]
    kchunk = jnp.arange(nc)[:, None] - WINDOW_CHUNKS + kj[None, :] // CHUNK
    valid = (kchunk >= 0)[:, None, :]
    out = sink_attention(qc, kb, vb, dist, valid, sinks)
    return out.reshape(b, length, ATTN_Q_DIM)


def swa_sample(q, k, v, k_cache, v_cache, sinks):
    b, length = q.shape[:2]
    rows = k_cache.shape[1]
    kk = jnp.concatenate([k_cache.astype(k.dtype), k], axis=1)[:, None]
    vv = jnp.concatenate([v_cache.astype(v.dtype), v], axis=1)[:, None]
    qpos = PAST_LEN + jnp.arange(length)
    kpos = PAST_LEN - rows + jnp.arange(rows + length)
    qch, kch = qpos // CHUNK, kpos // CHUNK
    valid = (kch[None, :] <= qch[:, None]) & (kch[None, :] >= qch[:, None] - WINDOW_CHUNKS)
    dist = jnp.abs(qpos[:, None] - kpos[None, :]).astype(jnp.float32)
    qr = q.reshape(b, 1, length, ATTN_KV_HEADS, ATTN_GROUP, HEAD_DIM)
    out = sink_attention(qr, kk, vv, dist[None], valid[None], sinks)
    return out.reshape(b, length, ATTN_Q_DIM)


def ssd_scan(x, dt, a, bm, cm, h0):
    f32 = jnp.float32
    b, length = x.shape[:2]
    q = SSD_CHUNK if length % SSD_CHUNK == 0 else length
    nc = length // q
    xc = x.astype(f32).reshape(b, nc, q, SSD_GROUPS, SSD_HPG, SSD_HEAD_DIM)
    dtc = dt.astype(f32).reshape(b, nc, q, SSD_GROUPS, SSD_HPG)
    bc = bm.astype(f32).reshape(b, nc, q, SSD_GROUPS, SSD_STATE)
    cc = cm.astype(f32).reshape(b, nc, q, SSD_GROUPS, SSD_STATE)
    cum = jnp.cumsum(dtc * a.astype(f32).reshape(SSD_GROUPS, SSD_HPG), axis=2)
    xdt = xc * dtc[..., None]
    cum_t = jnp.moveaxis(cum, 2, -1)
    seg = cum_t[..., :, None] - cum_t[..., None, :]
    causal = jnp.tril(jnp.ones((q, q), dtype=bool))
    lmat = jnp.exp(jnp.where(causal, seg, -jnp.inf))
    cb = jnp.einsum('bcqgn,bcsgn->bcgqs', cc, bc)
    y_diag = jnp.einsum('bcgjqs,bcsgjp->bcqgjp', cb[:, :, :, None] * lmat, xdt)
    decay_end = jnp.exp(cum[:, :, -1:] - cum)
    chunk_state = jnp.einsum('bcsgn,bcsgjp->bcgjpn', bc, xdt * decay_end[..., None])
    chunk_decay = jnp.exp(cum[:, :, -1])

    def step(h, inp):
        dec, st = inp
        return dec[..., None, None] * h + st, h

    h_init = h0.astype(f32).reshape(b, SSD_GROUPS, SSD_HPG, SSD_HEAD_DIM, SSD_STATE)
    h_final, h_start = lax.scan(step, h_init,
                                (jnp.moveaxis(chunk_decay, 1, 0), jnp.moveaxis(chunk_state, 1, 0)))
    h_start = jnp.moveaxis(h_start, 0, 1)
    y_off = jnp.einsum('bcqgn,bcgjpn->bcqgjp', cc, h_start) * jnp.exp(cum)[..., None]
    y = (y_diag + y_off).reshape(b, length, SSD_HEADS, SSD_HEAD_DIM)
    return y, h_final.reshape(b, SSD_HEADS, SSD_HEAD_DIM, SSD_STATE)


def mixer_swa_ssd(h, w_in, w_out, sinks, conv_w, conv_b, dt_bias, a_log, d_skip, norm_g, cache):
    f32 = jnp.float32
    b, length, _ = h.shape
    i1 = ATTN_Q_DIM
    i2 = i1 + ATTN_KV_DIM
    i3 = i2 + ATTN_KV_DIM
    i4 = i3 + SSD_INNER
    i5 = i4 + SSD_CONV_DIM
    q, k, v, z, xbc, dt = jnp.split(h @ w_in, [i1, i2, i3, i4, i5], axis=-1)
    k = k.reshape(b, length, ATTN_KV_HEADS, HEAD_DIM)
    v = v.reshape(b, length, ATTN_KV_HEADS, HEAD_DIM)
    if cache is None:
        attn = swa_prompt(q, k, v, sinks)
        conv_prev = jnp.zeros((b, SSD_CONV - 1, SSD_CONV_DIM), h.dtype)
        h0 = jnp.zeros((b, SSD_HEADS, SSD_HEAD_DIM, SSD_STATE), f32)
        new_k, new_v = k[:, -WINDOW:], v[:, -WINDOW:]
    else:
        k_cache, v_cache, h0, conv_prev = cache
        attn = swa_sample(q, k, v, k_cache, v_cache, sinks)
        new_k, new_v = k, v
    xbc, new_conv = causal_dwconv(xbc, conv_prev, conv_w)
    xbc = jax.nn.silu(xbc + conv_b)
    xs, bm, cm = jnp.split(xbc, [SSD_INNER, SSD_INNER + SSD_GROUPS * SSD_STATE], axis=-1)
    xs = xs.reshape(b, length, SSD_HEADS, SSD_HEAD_DIM)
    dt = jax.nn.softplus((dt + dt_bias).astype(f32))
    a = -jnp.exp(a_log.astype(f32))
    y, h_new = ssd_scan(xs, dt, a, bm.reshape(b, length, SSD_GROUPS, SSD_STATE),
                        cm.reshape(b, length, SSD_GROUPS, SSD_STATE), h0)
    y = y + d_skip.astype(f32)[:, None] * xs.astype(f32)
    y = y.reshape(b, length, SSD_GROUPS, SSD_INNER // SSD_GROUPS) * \
        jax.nn.silu(z.astype(f32)).reshape(b, length, SSD_GROUPS, SSD_INNER // SSD_GROUPS)
    y = y * lax.rsqrt(jnp.mean(y * y, axis=-1, keepdims=True) + EPS)
    y = (y.reshape(b, length, SSD_INNER) * norm_g.astype(f32)).astype(h.dtype)
    out = jnp.concatenate([attn.astype(h.dtype), y], axis=-1) @ w_out
    return out, (new_k, new_v, h_new.astype(h.dtype), new_conv)


def mixer_sconv(h, w_in, conv_w, w_out, cache):
    b = h.shape[0]
    gate_b, gate_c, xi = jnp.split(h @ w_in, 3, axis=-1)
    prev = jnp.zeros((b, SCONV_WIDTH - 1, D_MODEL), h.dtype) if cache is None else cache
    u, new_buf = causal_dwconv(gate_c * xi, prev, conv_w)
    return (gate_b * u) @ w_out, new_buf


def setup_inputs(seed: int = 0) -> dict:
    key = jax.random.key(seed)
    ks = jax.random.split(key, 32)
    nrm = jax.random.normal
    f32 = jnp.float32
    swa_rows = min(WINDOW, PAST_LEN)
    dt0 = jnp.exp(jax.random.uniform(ks[18], (N_EVEN, SSD_HEADS), f32, np.log(1e-3), np.log(1e-1)))
    return {
        "x_prompt": nrm(ks[0], (BATCH, SEQ, D_MODEL), f32),
        "x_sample": nrm(ks[1], (DEC_BATCH, DEC_SEQ, D_MODEL), f32),
        "c_prompt": nrm(ks[2], (BATCH, D_MODEL), f32),
        "c_sample": nrm(ks[3], (DEC_BATCH, D_MODEL), f32),
        "cache_swa_k": nrm(ks[4], (N_EVEN, DEC_BATCH, swa_rows, ATTN_KV_HEADS, HEAD_DIM), f32),
        "cache_swa_v": nrm(ks[5], (N_EVEN, DEC_BATCH, swa_rows, ATTN_KV_HEADS, HEAD_DIM), f32),
        "state_ssd": 0.5 * nrm(ks[6], (N_EVEN, DEC_BATCH, SSD_HEADS, SSD_HEAD_DIM, SSD_STATE), f32),
        "state_ssd_conv": nrm(ks[7], (N_EVEN, DEC_BATCH, SSD_CONV - 1, SSD_CONV_DIM), f32),
        "state_sconv": nrm(ks[8], (N_ODD, DEC_BATCH, SCONV_WIDTH - 1, D_MODEL), f32),
        "norm_g": 1.0 + 0.02 * nrm(ks[9], (DEPTH, 3, D_MODEL), f32),
        "w_ada": 0.3 * D_MODEL ** -0.5 * nrm(ks[10], (DEPTH, D_MODEL, N_MOD * D_MODEL), f32),
        "b_ada": 0.02 * nrm(ks[11], (DEPTH, N_MOD * D_MODEL), f32),
        "w_ffn_gate": D_MODEL ** -0.5 * nrm(ks[12], (DEPTH, 2, D_MODEL, D_FF), f32),
        "w_ffn_up": D_MODEL ** -0.5 * nrm(ks[13], (DEPTH, 2, D_MODEL, D_FF), f32),
        "w_ffn_down": D_FF ** -0.5 * nrm(ks[14], (DEPTH, 2, D_FF, D_MODEL), f32),
        "w_in_mix0": D_MODEL ** -0.5 * nrm(ks[15], (N_EVEN, D_MODEL, IN0_DIM), f32),
        "w_out_mix0": MIX0_DIM ** -0.5 * nrm(ks[16], (N_EVEN, MIX0_DIM, D_MODEL), f32),
        "attn_sinks": 0.5 * nrm(ks[17], (N_EVEN, ATTN_HEADS), f32),
        "ssd_conv_w": SSD_CONV ** -0.5 * nrm(ks[19], (N_EVEN, SSD_CONV, SSD_CONV_DIM), f32),
        "ssd_conv_b": 0.02 * nrm(ks[20], (N_EVEN, SSD_CONV_DIM), f32),
        "ssd_dt_bias": dt0 + jnp.log(-jnp.expm1(-dt0)),
        "ssd_a_log": jnp.log(jax.random.uniform(ks[21], (N_EVEN, SSD_HEADS), f32, 1.0, 16.0)),
        "ssd_d": 1.0 + 0.02 * nrm(ks[22], (N_EVEN, SSD_HEADS), f32),
        "ssd_norm_g": 1.0 + 0.02 * nrm(ks[23], (N_EVEN, SSD_INNER), f32),
        "w_in_mix1": D_MODEL ** -0.5 * nrm(ks[24], (N_ODD, D_MODEL, 3 * D_MODEL), f32),
        "sconv_w": SCONV_WIDTH ** -0.5 * nrm(ks[25], (N_ODD, SCONV_WIDTH, D_MODEL), f32),
        "w_out_mix1": D_MODEL ** -0.5 * nrm(ks[26], (N_ODD, D_MODEL, D_MODEL), f32),
        "final_norm_g": 1.0 + 0.02 * nrm(ks[27], (D_MODEL,), f32),
    }


def reference(x_prompt, x_sample, c_prompt, c_sample, cache_swa_k, cache_swa_v, state_ssd,
              state_ssd_conv, state_sconv, norm_g, w_ada, b_ada, w_ffn_gate, w_ffn_up, w_ffn_down,
              w_in_mix0, w_out_mix0, attn_sinks, ssd_conv_w, ssd_conv_b, ssd_dt_bias, ssd_a_log,
              ssd_d, ssd_norm_g, w_in_mix1, sconv_w, w_out_mix1, final_norm_g):

    def run(x, c, with_cache):
        swa_k, swa_v, ssd_h, ssd_cv, sconv = [], [], [], [], []
        for l in range(DEPTH):
            mod = jax.nn.silu(c) @ w_ada[l] + b_ada[l]
            sh1, sc1, g1, sh2, sc2, g2, sh3, sc3, g3 = [m[:, None] for m in jnp.split(mod, N_MOD, axis=-1)]
            h = rmsnorm(x, norm_g[l, 0]) * (1 + sc1) + sh1
            x = x + 0.5 * g1 * swiglu(h, w_ffn_gate[l, 0], w_ffn_up[l, 0], w_ffn_down[l, 0])
            h = rmsnorm(x, norm_g[l, 1]) * (1 + sc2) + sh2
            i = l // 2
            if l % 2 == 0:
                cache = (cache_swa_k[i], cache_swa_v[i], state_ssd[i], state_ssd_conv[i]) if with_cache else None
                m, (nk, nv, nh, ncv) = mixer_swa_ssd(h, w_in_mix0[i], w_out_mix0[i], attn_sinks[i], ssd_conv_w[i],
                                                     ssd_conv_b[i], ssd_dt_bias[i], ssd_a_log[i], ssd_d[i],
                                                     ssd_norm_g[i], cache)
                swa_k.append(nk)
                swa_v.append(nv)
                ssd_h.append(nh)
                ssd_cv.append(ncv)
            else:
                cache = state_sconv[i] if with_cache else None
                m, nb = mixer_sconv(h, w_in_mix1[i], sconv_w[i], w_out_mix1[i], cache)
                sconv.append(nb)
            x = x + g2 * m
            h = rmsnorm(x, norm_g[l, 2]) * (1 + sc3) + sh3
            x = x + 0.5 * g3 * swiglu(h, w_ffn_gate[l, 1], w_ffn_up[l, 1], w_ffn_down[l, 1])
        return (rmsnorm(x, final_norm_g), jnp.stack(swa_k), jnp.stack(swa_v), jnp.stack(ssd_h),
                jnp.stack(ssd_cv), jnp.stack(sconv))

    y_prompt, k_p, v_p, h_p, cv_p, sc_p = run(x_prompt, c_prompt, False)
    y_sample, k_s, v_s, h_s, cv_s, sc_s = run(x_sample, c_sample, True)
    return (y_prompt, y_sample, k_p, v_p, h_p, cv_p, sc_p, k_s, v_s, h_s, cv_s, sc_s)
```

```python
import numpy as np
from contextlib import ExitStack
import concourse.bass as bass
import concourse.mybir as mybir
from concourse.bass_utils import run_bass_kernel_spmd

F32 = mybir.dt.float32
BF16 = mybir.dt.bfloat16
ALU = mybir.AluOpType
AF = mybir.ActivationFunctionType
AX = mybir.AxisListType

D = 2048
DFF = 5632
NKC = 16
NFC = 44
SEQH = 4096
TT = 512
NTILE = SEQH // TT
EPS = 1e-6
NS = 8
NSLAB = 4
NCORES = 8


class Res:
    __slots__ = ("name", "w", "r")

    def __init__(self, name):
        self.name = name
        self.w = None
        self.r = []


class Op:
    __slots__ = ("eng", "fn", "deps", "inc", "dma", "semval")

    def __init__(self, eng, fn):
        self.eng = eng
        self.fn = fn
        self.deps = []
        self.inc = False
        self.dma = None
        self.semval = 0


class Prog:
    ENGS = ("pe", "act", "dve", "pool", "sp")

    def __init__(self, nc):
        self.nc = nc
        self.ops = {e: [] for e in self.ENGS}
        self.ndma = {e: 0 for e in self.ENGS}

    def op(self, eng, fn, reads=(), writes=(), dma=False):
        ops = self.ops[eng]
        rec = Op(eng, fn)
        idx = len(ops)
        j = 0
        if dma:
            j = self.ndma[eng]
            self.ndma[eng] = j + 1
            rec.dma = j
            tok = ("d", eng, j)
        else:
            tok = ("e", eng, idx)
        deps = {}
        for r in reads:
            if r.w is not None:
                deps[r.w] = "raw"
        for w in writes:
            if w.w is not None and w.w not in deps:
                deps[w.w] = "waw"
            for t in w.r:
                if t not in deps:
                    deps[t] = "war"
        for t, kind in deps.items():
            if t == tok:
                continue
            if t[0] == "e" and t[1] == eng:
                if eng == "pe":
                    continue
                if kind != "raw" and not dma:
                    continue
            rec.deps.append(t)
            if t[0] == "e":
                self.ops[t[1]][t[2]].inc = True
        if dma and j >= NS:
            rec.deps.append(("d", eng, j - NS))
        for r in reads:
            if tok[0] == "e":
                r.r = [t for t in r.r if not (t[0] == "e" and t[1] == eng)]
            r.r.append(tok)
        for w in writes:
            w.w = tok
            w.r = []
        ops.append(rec)
        return rec

    def barrier(self):
        last = []
        for e in self.ENGS:
            for i in range(len(self.ops[e]) - 1, -1, -1):
                o = self.ops[e][i]
                if o.dma is None and o.fn is not None:
                    last.append(("e", e, i))
                    o.inc = True
                    break
            n = self.ndma[e]
            for j in range(max(0, n - NS), n):
                last.append(("d", e, j))
        for e in self.ENGS:
            rec = Op(e, None)
            rec.deps = list(last)
            self.ops[e].append(rec)

    def emit(self, stack):
        nc = self.nc
        self.esem = {e: stack.enter_context(nc.semaphore("es_" + e)) for e in self.ENGS}
        self.dsem = {
            e: [stack.enter_context(nc.semaphore("ds_%s%d" % (e, i))) for i in range(NS)]
            for e in self.ENGS
            if self.ndma[e] > 0
        }
        for e in self.ENGS:
            c = 0
            for o in self.ops[e]:
                if o.inc:
                    c += 1
                o.semval = c
        block = stack.enter_context(nc.Block())
        engs = {"pe": block.tensor, "act": block.scalar, "dve": block.vector,
                "pool": block.gpsimd, "sp": block.sync}
        for e in self.ENGS:
            def body(eng, e=e):
                self._emit_engine(e, eng)
            engs[e](body)

    def _emit_engine(self, e, eng):
        waited = {}
        for o in self.ops[e]:
            for t in o.deps:
                if t[0] == "e":
                    key = ("e", t[1])
                    sem = self.esem[t[1]]
                    val = self.ops[t[1]][t[2]].semval
                else:
                    key = ("d", t[1], t[2] % NS)
                    sem = self.dsem[t[1]][t[2] % NS]
                    val = 16 * (t[2] // NS + 1)
                if waited.get(key, 0) >= val:
                    continue
                eng.wait_ge(sem, val)
                waited[key] = val
            if o.fn is None:
                continue
            ins = o.fn(eng)
            if o.dma is not None:
                ins.then_inc(self.dsem[e][o.dma % NS], 16)
            elif o.inc:
                ins.then_inc(self.esem[e], 1)
        n = self.ndma[e]
        for j in range(max(0, n - NS), n):
            key = ("d", e, j % NS)
            val = 16 * (j // NS + 1)
            if waited.get(key, 0) >= val:
                continue
            eng.wait_ge(self.dsem[e][j % NS], val)
            waited[key] = val


SLAB_E = 4096
TM = 256
WGROUPS = [
    ("gu00", 88, 2048), ("dn00", 32, 2816), ("in0", 16, 4096), ("out0", 8, 4096),
    ("gu01", 88, 2048), ("dn01", 32, 2816), ("gu10", 88, 2048), ("dn10", 32, 2816),
    ("in1", 24, 4096), ("out1", 8, 4096), ("gu11", 88, 2048), ("dn11", 32, 2816),
]
NEG = -30000.0
BG_GROUPS = ("out0", "gu01", "dn01", "gu10", "dn10", "in1", "out1", "gu11", "dn11")
BG_W = 1024


def full(t):
    return t[tuple(slice(None) for _ in t.shape)]


class K:
    pass


class StopBuild(Exception):
    pass


def chkf(k, n):
    import os
    if float(os.environ.get("KFULL", "99")) <= n:
        raise StopBuild()


def chk(k, n):
    if getattr(k, "level", 99) <= n:
        raise StopBuild()


def group_on(name, layers, ffn_on):
    if name.startswith("gu") or name.startswith("dn"):
        return ffn_on and int(name[2]) in layers
    return int(name[-1]) in layers


def build(ntile=NTILE, layers=(0, 1), ffn_on=True, npre=None):
    nc = bass.Bass("TRN2", target_bir_lowering=False)
    k = K()
    k.nc = nc
    k.ntile = ntile
    k.layers = layers
    k.ffn_on = ffn_on
    seqh = ntile * TT
    k.seqh = seqh
    P = Prog(nc)
    k.P = P

    def din(name, shape, dt=F32):
        return nc.dram_tensor(name, list(shape), dt, kind="ExternalInput").ap()

    def dout(name, shape, dt=F32):
        return nc.dram_tensor(name, list(shape), dt, kind="ExternalOutput").ap()

    def dscr(name, shape, dt):
        return nc.dram_tensor(name, list(shape), dt).ap()

    k.groups = [g for g in WGROUPS if group_on(g[0], layers, ffn_on)]
    I = {}
    I["xpT"] = din("xpT", [128, 16, seqh])
    I["xqT"] = din("xqT", [128, 16, seqh])
    I["xsT"] = din("xsT", [128, 16, 16])
    I["c2T"] = din("c2T", [128, 16, 2])
    I["flag"] = din("flag", [128, 1])
    I["ident"] = din("ident", [128, 128])
    I["cmask"] = din("cmask", [128, 5, 128])
    I["alibi"] = din("alibi", [64, 3 * 512])
    I["norm_gT"] = din("norm_gT", [128, 6, 16])
    I["fnormT"] = din("fnormT", [128, 16])
    I["b_adaT"] = din("b_adaT", [128, 288])
    I["w_ada"] = din("w_ada", [144, 128, 4096])
    I["sinks2"] = din("sinks2", [128, 8])
    I["dtb_b"] = din("dtb_b", [128, 16])
    I["alog_b"] = din("alog_b", [128, 16])
    I["d_b"] = din("d_b", [128, 16])
    I["ng_b"] = din("ng_b", [128, 1024])
    I["conv_wT"] = din("conv_wT", [128, 12, 4])
    I["conv_bT"] = din("conv_bT", [128, 12])
    I["sconv_wT"] = din("sconv_wT", [128, 16, 3])
    I["ckT"] = din("ckT", [128, 128])
    I["cvc"] = din("cvc", [64, 2, 128])
    I["st_hT"] = din("st_hT", [128, 1024])
    I["st_cvT"] = din("st_cvT", [128, 12, 3])
    I["st_scT"] = din("st_scT", [128, 16, 2])
    for name, n, e in k.groups:
        I["w_" + name] = din("w_" + name, [n, 128, e])
    k.I = I
    O = {}
    O["ypT"] = dout("ypT", [128, 16, seqh])
    O["ysT"] = dout("ysT", [128, 16, 16])
    for pfx in ("p", "s"):
        nk = 128 if pfx == "p" else 16
        O["k" + pfx] = dout("o_k" + pfx, [nk, 128])
        O["v" + pfx] = dout("o_v" + pfx, [nk, 128])
        O["h" + pfx] = dout("o_h" + pfx, [128, 1024])
        O["cv" + pfx] = dout("o_cv" + pfx, [128, 12, 3])
        O["sc" + pfx] = dout("o_sc" + pfx, [128, 16, 2])
    import os as _os
    k.dbg = _os.environ.get("KDBG", "") == "1"
    if k.dbg:
        O["d_attnT"] = dout("d_attnT", [128, 8, TM], BF16)
        O["d_yT"] = dout("d_yT", [128, 8, TM], BF16)
        O["d_KT"] = dout("d_KT", [128, 128 + TM], BF16)
        O["d_QrT"] = dout("d_QrT", [128, 8, TM], BF16)
        O["d_kv"] = dout("d_kv", [128, 256], F32)
    k.O = O
    S = {}
    for name, n, e in k.groups:
        S["b_" + name] = dscr("b_" + name, [n * 128, e], BF16)
    k.S = S
    k.wres = {name: [[Res("w_%s_%d" % (name, j))] for j in range(n)] for name, n, e in k.groups}
    k.bg_names = set(BG_GROUPS) if (0 in layers and ffn_on) else set()
    k.bg_jobs, k.bg_pos, k.bg_on, k.bg_slot, k.bg_slots = [], 0, False, 0, 1
    k.wE = {name: e for name, n, e in k.groups}

    with ExitStack() as st:
        k.st = st

        def sb(name, shape, dt):
            return st.enter_context(nc.sbuf_tensor("s_" + name, list(shape), dt))

        def pst(name, shape, dt=F32):
            return st.enter_context(nc.psum_tensor(name, list(shape), dt))

        k.xT = sb("xT", [128, 16, TT], F32)
        k.r_xT = [Res("xT%d" % c) for c in range(16)]
        k.hT = sb("hT", [128, 16, TT], BF16)
        k.r_hT = [Res("hT%d" % c) for c in range(16)]
        k.big = sb("big", [128, NFC * TT], BF16)
        k.r_act = [Res("act%d" % f) for f in range(NFC)]
        k.slab = [sb("slab%d" % j, [128, SLAB_E], BF16) for j in range(NSLAB)]
        k.r_slab = [Res("slab%d" % j) for j in range(NSLAB)]
        k.slab_i = 0
        k.gs = [sb("gs%d" % j, [128, TT], F32) for j in range(4)]
        k.r_gs = [Res("gs%d" % j) for j in range(4)]
        k.sq = k.gs[0:2]
        k.r_sq = k.r_gs[0:2]
        k.tmp = k.gs[0:2]
        k.r_tmp = k.r_gs[0:2]
        k.sgt = k.gs[2:4]
        k.r_sgt = k.r_gs[2:4]
        k.rt = k.gs[2]
        k.r_rt = k.r_gs[2]
        k.rstd = k.gs[3]
        k.r_rstd = k.r_gs[3]
        k.ident = sb("ident", [128, 128], F32)
        k.identb = sb("identb", [128, 128], BF16)
        k.ones = sb("ones", [128, 128], F32)
        k.cmask = sb("cmask", [128, 5, 128], F32)
        k.alibi = sb("alibi", [64, 3 * 512], F32)
        k.modT = sb("modT", [128, 288, 2], F32)
        k.A = sb("A", [128, 6, 16, 2], F32)
        k.SH = sb("SH", [128, 6, 16, 2], F32)
        k.G = sb("G", [128, 6, 16, 2], F32)
        k.norm_gT = sb("norm_gT", [128, 6, 16], F32)
        k.fnormT = sb("fnormT", [128, 16], F32)
        k.b_adaT = sb("b_adaT", [128, 288], F32)
        k.c2T = sb("c2T", [128, 16, 2], F32)
        k.scT = sb("scT", [128, 16, 2], BF16)
        k.flag = sb("flag", [128, 1], F32)
        k.hm = sb("hm", [128, 1], F32)
        k.esink = sb("esink", [128, 8], F32)
        k.dtb = sb("dtb", [128, 16], F32)
        k.aneg = sb("aneg", [128, 16], F32)
        k.d16 = sb("d16", [128, 16], F32)
        k.DT = sb("DT", [128, 1024], F32)
        k.NG = sb("NG", [128, 1024], F32)
        k.cwT = sb("cwT", [128, 12, 4], F32)
        k.cbT = sb("cbT", [128, 12], F32)
        k.swT = sb("swT", [128, 16, 3], F32)
        k.onesk = [sb("onesk%d" % j, [64, 128], BF16) for j in range(2)]
        k.r_const = Res("const")
        k.KT = sb("KT", [128, 128 + TM], BF16)
        k.r_KT = Res("KT")
        k.Va = [sb("Va%d" % j, [64, 2 + TM // 64, 128], BF16) for j in range(2)]
        k.r_Va = Res("Va")
        k.hs = sb("hs", [128, 1024], F32)
        k.r_hs = Res("hs")
        k.hb = [sb("hb%d" % j, [128, 1024], BF16) for j in range(2)]
        k.r_hb = [Res("hb0"), Res("hb1")]
        k.phalo = sb("phalo", [128, 16, 2], F32)
        k.r_phalo = Res("phalo")
        k.cvhalo = sb("cvhalo", [128, 12, 3], F32)
        k.r_cvhalo = Res("cvhalo")
        k.ps = [pst("ps%d" % j, [128, 512]) for j in range(8)]
        k.r_ps = [Res("ps%d" % j) for j in range(8)]

        import os
        k.level = int(os.environ.get("KLEVEL", "99"))
        try:
            phase0(k)
            chk(k, 0)
            P.barrier()
            main_phase(k)
        except StopBuild:
            pass
        P.emit(st)
    return nc


def phase0(k):
    nc, P, I, S = k.nc, k.P, k.I, k.S
    rc = k.r_const
    loads = [(k.ident, "ident"), (k.norm_gT, "norm_gT"), (k.fnormT, "fnormT"), (k.b_adaT, "b_adaT"), (k.c2T, "c2T"),
             (k.flag, "flag"), (k.cmask, "cmask"), (k.alibi, "alibi"), (k.esink, "sinks2"), (k.dtb, "dtb_b"),
             (k.aneg, "alog_b"), (k.d16, "d_b"), (k.NG, "ng_b"), (k.cwT, "conv_wT"), (k.cbT, "conv_bT"), (k.swT, "sconv_wT")]
    for dst, nm in loads:
        P.op("sp", lambda e, dst=dst, nm=nm: e.dma_start(out=full(dst), in_=full(I[nm])), [], [rc], dma=True)
    P.op("pool", lambda e: e.memset(k.ones[:, :], 1.0), [], [rc])
    P.op("dve", lambda e: e.tensor_copy(out=k.identb[:, :], in_=k.ident[:, :]), [rc], [rc])
    P.op("act", lambda e: e.activation(out=k.scT[:, :, :], in_=k.c2T[:, :, :], func=AF.Silu), [rc], [rc])
    P.op("act", lambda e: e.activation(out=k.esink[:, :], in_=k.esink[:, :], func=AF.Exp), [rc], [rc])
    P.op("act", lambda e: e.activation(out=k.aneg[:, :], in_=k.aneg[:, :], func=AF.Exp), [rc], [rc])
    P.op("dve", lambda e: e.tensor_scalar(out=k.aneg[:, :], in0=k.aneg[:, :], scalar1=-1.0, scalar2=None, op0=ALU.mult), [rc], [rc])
    P.op("dve", lambda e: e.tensor_scalar(out=k.hm[:, :], in0=k.flag[:, :], scalar1=-1.0, scalar2=-NEG, op0=ALU.add, op1=ALU.mult),
         [rc], [rc])
    P.op("dve", lambda e: e.tensor_copy(out=k.DT[:, :].rearrange("p (j q) -> p j q", q=64),
                                        in_=k.d16[:, :].unsqueeze(2).to_broadcast([128, 16, 64])), [rc], [rc])
    for j in range(2):
        P.op("pool", lambda e, j=j: e.memset(k.onesk[j][:, :], 0.0), [], [rc])
        P.op("pool", lambda e, j=j: e.memset(k.onesk[j][:, j * 64:(j + 1) * 64], 1.0), [], [rc])
        P.op("pool", lambda e, j=j: e.memset(full(k.Va[j]), 0.0), [], [k.r_Va])
    P.op("pool", lambda e: e.memset(k.KT[:, :], 0.0), [], [k.r_KT])
    P.op("pool", lambda e: e.memset(k.hs[:, :], 0.0), [], [k.r_hs])
    P.op("pool", lambda e: e.memset(k.hb[0][:, :], 0.0), [], [k.r_hb[0]])
    P.op("pool", lambda e: e.memset(full(k.phalo), 0.0), [], [k.r_phalo])
    P.op("pool", lambda e: e.memset(full(k.cvhalo), 0.0), [], [k.r_cvhalo])

    stg_f = [k.xT[:, 0:8, :].rearrange("p a b -> p (a b)"), k.xT[:, 8:16, :].rearrange("p a b -> p (a b)")]
    r_stg_f = [Res("stgf0"), Res("stgf1")]
    stg_b = k.slab
    r_stg_b = k.r_slab
    cast_engs = ("dve", "pool", "act")
    n = 0

    def cast(eng, dst, src):
        if eng == "act":
            return lambda e: e.activation(out=dst, in_=src, func=AF.Copy)
        return lambda e: e.tensor_copy(out=dst, in_=src)

    pm = [k.ps[6], k.ps[7]]
    r_pm = [k.r_ps[6], k.r_ps[7]]
    for sl in range(144):
        if (sl // 72) not in k.layers:
            continue
        fb = n % 2
        bb = n % NSLAB
        P.op("sp", lambda e, fb=fb, sl=sl: e.dma_start(out=stg_f[fb], in_=I["w_ada"][sl]), [], [r_stg_f[fb]], dma=True)
        ce = cast_engs[n % 3]
        P.op(ce, cast(ce, stg_b[bb][:, :], stg_f[fb]), [r_stg_f[fb]], [r_stg_b[bb]])
        for ch in range(2):
            j = sl * 2 + ch
            pj = j % 2
            for kc in range(16):
                P.op("pe", lambda e, bb=bb, kc=kc, ch=ch, pj=pj: e.matmul(
                    pm[pj][:, 0:2], stg_b[bb][:, kc * 256 + ch * 128: kc * 256 + ch * 128 + 128],
                    k.scT[:, kc, :], start=(kc == 0), stop=(kc == 15)), [r_stg_b[bb], rc], [r_pm[pj]])
            P.op("act", lambda e, j=j, pj=pj: e.activation(out=k.modT[:, j, :], in_=pm[pj][:, 0:2], func=AF.Identity,
                                                          bias=k.b_adaT[:, j:j + 1], scale=1.0), [r_pm[pj], rc], [rc])
        n += 1
    for l in k.layers:
        for i in range(3):
            li = l * 3 + i
            j0 = l * 144 + (3 * i) * 16
            P.op("dve", lambda e, li=li, j0=j0: e.tensor_copy(out=k.SH[:, li, :, :], in_=k.modT[:, j0:j0 + 16, :]), [rc], [rc])
            P.op("dve", lambda e, li=li, j0=j0: e.tensor_scalar(out=k.A[:, li, :, :], in0=k.modT[:, j0 + 16:j0 + 32, :],
                                                                 scalar1=1.0, scalar2=None, op0=ALU.add), [rc], [rc])
            for sq_ in range(2):
                P.op("dve", lambda e, li=li, sq_=sq_: e.tensor_tensor(out=k.A[:, li, :, sq_], in0=k.A[:, li, :, sq_],
                                                                      in1=k.norm_gT[:, li, :], op=ALU.mult), [rc], [rc])
            gs = 1.0 if i == 1 else 0.5
            P.op("dve", lambda e, li=li, j0=j0, gs=gs: e.tensor_scalar(out=k.G[:, li, :, :], in0=k.modT[:, j0 + 32:j0 + 48, :],
                                                                        scalar1=gs, scalar2=None, op0=ALU.mult), [rc], [rc])
    for name, nsl, el in k.groups:
        if name in k.bg_names:
            continue
        per = max(1, SLAB_E // el)
        s = 0
        while s < nsl:
            cnt = min(per, nsl - s)
            fb = n % 2
            bb = n % NSLAB
            src = I["w_" + name][s:s + cnt].rearrange("s p e -> p s e")
            dstd = S["b_" + name].rearrange("(s p) e -> p s e", p=128)[:, s:s + cnt, :]
            sf = stg_f[fb][:, 0:cnt * el]
            sbf = stg_b[bb][:, 0:cnt * el]
            P.op("sp", lambda e, sf=sf, src=src, cnt=cnt: e.dma_start(out=sf.rearrange("p (s e) -> p s e", s=cnt), in_=src),
                 [], [r_stg_f[fb]], dma=True)
            ce = cast_engs[n % 3]
            P.op(ce, cast(ce, sbf, sf), [r_stg_f[fb]], [r_stg_b[bb]])
            P.op("pool", lambda e, sbf=sbf, dstd=dstd, cnt=cnt: e.dma_start(out=dstd, in_=sbf.rearrange("p (s e) -> p s e", s=cnt)),
                 [r_stg_b[bb]], [r for j in range(s, s + cnt) for r in k.wres[name][j]], dma=True)
            n += 1
            s += cnt


def load_slab(k, name, idx, cnt=1):
    P = k.P
    j = k.slab_i % NSLAB
    k.slab_i += 1
    el = k.wE[name]
    src = k.S["b_" + name].rearrange("(s p) e -> p s e", p=128)[:, idx:idx + cnt, :]
    buf = k.slab[j]
    P.op("sp", lambda e: e.dma_start(out=buf[:, 0:cnt * el].rearrange("p (s e) -> p s e", s=cnt), in_=src),
         [r for i in range(idx, idx + cnt) for r in k.wres[name][i]], [k.r_slab[j]], dma=True)
    return buf, k.r_slab[j]


def make_bg_jobs(k):
    jobs = []
    for name, nsl, el in k.groups:
        if name not in k.bg_names:
            continue
        for sidx in range(nsl):
            a = 0
            pieces = []
            while a < el:
                b = min(el, a + BG_W)
                r = Res("w_%s_%d_%d" % (name, sidx, a))
                pieces.append(r)
                jobs.append((name, sidx, a, b, r))
                a = b
            k.wres[name][sidx] = pieces
    k.bg_jobs = jobs
    k.bg_pos = 0


def bg_step(k, n):
    P, W = k.P, k.wk
    for _ in range(n):
        if k.bg_pos >= len(k.bg_jobs):
            return
        name, sidx, a, b, r = k.bg_jobs[k.bg_pos]
        i = k.bg_pos % 2
        k.bg_pos += 1
        w = b - a
        sf, rsf = ((W.y, W.r_y), (W.xs, W.r_xs))[i]
        sbf, rsb = ((W.MT[:, 0:8, :].rearrange("p a b -> p (a b)"), W.r_MT), (W.yn, W.r_yn))[i]
        src = k.I["w_" + name][sidx][:, a:b]
        dst = k.S["b_" + name][sidx * 128:(sidx + 1) * 128, a:b]
        P.op("sp", lambda e, sf=sf, src=src, w=w: e.dma_start(out=sf[:, 0:w], in_=src), [], [rsf], dma=True)
        P.op("pool", lambda e, sf=sf, sbf=sbf, w=w: e.tensor_copy(out=sbf[:, 0:w], in_=sf[:, 0:w]), [rsf], [rsb])
        P.op("pool", lambda e, sbf=sbf, dst=dst, w=w: e.dma_start(out=dst, in_=sbf[:, 0:w]), [rsb], [r], dma=True)


def bg_tick(k):
    if not k.bg_on:
        return
    k.bg_slot += 1
    target = -(-k.bg_slot * len(k.bg_jobs) // k.bg_slots)
    bg_step(k, max(0, min(target, len(k.bg_jobs)) - k.bg_pos))


def norm_mod(k, li, seq, Tn, final=False):
    P = k.P
    rc = k.r_const
    pss, r_pss = k.ps[6], k.r_ps[6]
    for c in range(16):
        q = c % 2
        P.op("act", lambda e, c=c, q=q: e.activation(out=k.sq[q][:, 0:Tn], in_=k.xT[:, c, 0:Tn], func=AF.Square),
             [k.r_xT[c]], [k.r_sq[q]])
        P.op("pe", lambda e, c=c, q=q: e.matmul(pss[:, 0:Tn], k.ones[:, :], k.sq[q][:, 0:Tn], start=(c == 0), stop=(c == 15)),
             [k.r_sq[q], rc], [r_pss])
    P.op("act", lambda e: e.activation(out=k.rt[:, 0:Tn], in_=pss[:, 0:Tn], func=AF.Sqrt, bias=EPS, scale=1.0 / D),
         [r_pss], [k.r_rt])
    P.op("dve", lambda e: e.reciprocal(out=k.rstd[:, 0:Tn], in_=k.rt[:, 0:Tn]), [k.r_rt], [k.r_rstd])
    for c in range(16):
        q = c % 2
        if final:
            P.op("dve", lambda e, c=c: e.scalar_tensor_tensor(out=k.xT[:, c, 0:Tn], in0=k.xT[:, c, 0:Tn], scalar=k.fnormT[:, c:c + 1],
                                                              in1=k.rstd[:, 0:Tn], op0=ALU.mult, op1=ALU.mult),
                 [k.r_xT[c], k.r_rstd, rc], [k.r_xT[c]])
        else:
            P.op("dve", lambda e, c=c, q=q: e.scalar_tensor_tensor(out=k.tmp[q][:, 0:Tn], in0=k.xT[:, c, 0:Tn],
                                                                   scalar=k.A[:, li, c, seq:seq + 1], in1=k.rstd[:, 0:Tn],
                                                                   op0=ALU.mult, op1=ALU.mult),
                 [k.r_xT[c], k.r_rstd, rc], [k.r_tmp[q]])
            P.op("act", lambda e, c=c, q=q: e.activation(out=k.hT[:, c, 0:Tn], in_=k.tmp[q][:, 0:Tn], func=AF.Identity,
                                                         bias=k.SH[:, li, c, seq:seq + 1], scale=1.0),
                 [k.r_tmp[q], rc], [k.r_hT[c]])


def ffn(k, l, i, seq, Tn):
    P = k.P
    rc = k.r_const
    if not k.ffn_on:
        return
    li = l * 3 + (0 if i == 0 else 2)
    actT = k.big
    gname = "gu%d%d" % (l, i)
    dname = "dn%d%d" % (l, i)
    for s in range(22):
        bg, rg = load_slab(k, gname, 2 * s, 2)
        bu, ru = load_slab(k, gname, 44 + 2 * s, 2)
        for ch in range(2):
            f = 2 * s + ch
            q = f % 2
            pg, rpg = k.ps[q], k.r_ps[q]
            pu, rpu = k.ps[2 + q], k.r_ps[2 + q]
            for (w, rw, pp, rpp) in ((bg, rg, pg, rpg), (bu, ru, pu, rpu)):
                for kc in range(16):
                    P.op("pe", lambda e, w=w, pp=pp, kc=kc, ch=ch: e.matmul(
                        pp[:, 0:Tn], w[:, ch * 2048 + kc * 128: ch * 2048 + kc * 128 + 128], k.hT[:, kc, 0:Tn],
                        start=(kc == 0), stop=(kc == 15)), [rw, k.r_hT[kc]], [rpp])
            P.op("act", lambda e, q=q, pg=pg: e.activation(out=k.sgt[q][:, 0:Tn], in_=pg[:, 0:Tn], func=AF.Silu),
                 [rpg], [k.r_sgt[q]])
            P.op("dve", lambda e, q=q, pu=pu, f=f: e.tensor_tensor(out=actT[:, f * TT: f * TT + Tn], in0=k.sgt[q][:, 0:Tn],
                                                                   in1=pu[:, 0:Tn], op=ALU.mult),
                 [k.r_sgt[q], rpu, k.r_cvhalo], [k.r_act[f]])
        bg_tick(k)
    for dc in range(16):
        q = dc % 2
        pd, rpd = k.ps[4 + q], k.r_ps[4 + q]
        for half in range(2):
            bd, rd = load_slab(k, dname, dc * 2 + half, 1)
            for kc in range(22):
                f = half * 22 + kc
                P.op("pe", lambda e, bd=bd, pd=pd, kc=kc, f=f, half=half: e.matmul(
                    pd[:, 0:Tn], bd[:, kc * 128:(kc + 1) * 128], actT[:, f * TT: f * TT + Tn],
                    start=(half == 0 and kc == 0), stop=(half == 1 and kc == 21)), [rd, k.r_act[f]], [rpd])
        P.op("dve", lambda e, dc=dc, pd=pd: e.scalar_tensor_tensor(out=k.xT[:, dc, 0:Tn], in0=pd[:, 0:Tn],
                                                                   scalar=k.G[:, li, dc, seq:seq + 1], in1=k.xT[:, dc, 0:Tn],
                                                                   op0=ALU.mult, op1=ALU.add),
             [rpd, k.r_xT[dc], rc], [k.r_xT[dc]])
        bg_tick(k)


def carve(k):
    if hasattr(k, "cv"):
        return k.cv
    cv = K()
    off = [0]

    def take(n_bf16):
        a = off[0]
        off[0] += n_bf16
        assert off[0] <= NFC * TT, off[0]
        return k.big[:, a:a + n_bf16]

    cv.QrT = take(8 * TM).rearrange("p (g t) -> p g t", g=8)
    cv.attnT = take(8 * TM).rearrange("p (g t) -> p g t", g=8)
    cv.yT = take(8 * TM).rearrange("p (g t) -> p g t", g=8)
    cv.xbcT = take(2 * 12 * (TM + 4)).bitcast(F32).rearrange("p (c t) -> p c t", c=12)
    cv.dtraw = take(2 * 2 * 16).bitcast(F32).rearrange("p (b j) -> p b j", b=2)
    cv.sz = take(2 * 2 * 1024).bitcast(F32).rearrange("p (b j) -> p b j", b=2)
    cv.R = take(2 * 8 * 128).bitcast(F32).rearrange("p (j q) -> p j q", j=8)
    cv.Lh = take(2 * 8 * 128).bitcast(F32).rearrange("p (j q) -> p j q", j=8)
    cv.r_QrT, cv.r_attnT, cv.r_yT, cv.r_xbcT, cv.r_dtraw = Res("QrT"), Res("attnT"), Res("yT"), Res("xbcT"), Res("dtraw")
    cv.r_sz, cv.r_R, cv.r_Lh = Res("sz"), Res("R"), Res("Lh")
    k.cv = cv
    return cv


def mixer0(k, seq, Tn, c0, kind, first_main):
    P = k.P
    rc = k.r_const
    cv = carve(k)
    st = k.st_m0
    li = 1
    pre = kind == "pre"
    nblk = (Tn + 127) // 128
    Lc = 64 if Tn >= 64 else Tn
    nch = Tn // Lc
    pq = [0]

    def bank():
        j = pq[0] % 4
        pq[0] += 1
        return k.ps[j], k.r_ps[j]

    def fm_group(buf, rbuf, coff, evac):
        pp, rpp = bank()
        for kc in range(16):
            P.op("pe", lambda e, kc=kc, pp=pp: e.matmul(pp[:, 0:Tn], buf[:, kc * 256 + coff: kc * 256 + coff + 128],
                                                        k.hT[:, kc, c0:c0 + Tn], start=(kc == 0), stop=(kc == 15)),
                 [rbuf, k.r_hT[kc]], [rpp])
        evac(pp, rpp)

    P.op("pool", lambda e: e.tensor_copy(out=cv.xbcT[:, :, 0:3], in_=full(k.cvhalo)), [k.r_cvhalo, k.r_hT[15]], [cv.r_xbcT])
    if not pre:
        for s in range(4):
            buf, rbuf = load_slab(k, "in0", s)
            for ch in range(2):
                g = 2 * s + ch
                fm_group(buf, rbuf, ch * 128, lambda pp, rpp, g=g: P.op(
                    "act", lambda e: e.activation(out=cv.QrT[:, g, 0:Tn], in_=pp[:, 0:Tn], func=AF.Copy, scale=0.125),
                    [rpp], [cv.r_QrT]))
    buf, rbuf = load_slab(k, "in0", 4)
    fm_group(buf, rbuf, 0, lambda pp, rpp: P.op(
        "act", lambda e: e.activation(out=k.KT[:, 128:128 + Tn], in_=pp[:, 0:Tn], func=AF.Copy), [rpp], [k.r_KT]))
    for c in range(nch):
        pp, rpp = bank()
        for kc in range(16):
            P.op("pe", lambda e, kc=kc, pp=pp, c=c, buf=buf: e.matmul(pp[0:Lc, 0:128], k.hT[:, kc, c0 + c * Lc:c0 + (c + 1) * Lc],
                                                            buf[:, kc * 256 + 128: kc * 256 + 256], start=(kc == 0), stop=(kc == 15)),
                 [rbuf, k.r_hT[kc]], [rpp])
        for j in range(2):
            P.op("act", lambda e, pp=pp, c=c, j=j: e.activation(out=k.Va[j][0:Lc, 2 + c, j * 64:(j + 1) * 64],
                                                                in_=pp[0:Lc, j * 64:(j + 1) * 64], func=AF.Copy), [rpp], [k.r_Va])
    if st.get("kv_out") is not None and c0 + Tn >= st["Tn"]:
        nk = min(128, Tn)
        t0 = Tn - nk
        pp, rpp = bank()
        for kc in range(16):
            P.op("pe", lambda e, kc=kc, pp=pp, buf=buf: e.matmul(pp[0:nk, 0:256], k.hT[:, kc, c0 + t0:c0 + Tn], buf[:, kc * 256: kc * 256 + 256],
                                                        start=(kc == 0), stop=(kc == 15)), [rbuf, k.r_hT[kc]], [rpp])
        P.op("act", lambda e, pp=pp: e.activation(out=k.sgt[0][0:nk, 0:256], in_=pp[0:nk, 0:256], func=AF.Copy), [rpp], [k.r_sgt[0]])
        ko, vo = st["kv_out"]
        P.op("pool", lambda e: e.dma_start(out=ko[:, :], in_=k.sgt[0][0:nk, 0:128]), [k.r_sgt[0]], [], dma=True)
        P.op("pool", lambda e: e.dma_start(out=vo[:, :], in_=k.sgt[0][0:nk, 128:256]), [k.r_sgt[0]], [], dma=True)
    if not pre:
        for s in range(4):
            buf, rbuf = load_slab(k, "in0", 5 + s)
            for b in range(nblk):
                bs = min(128, Tn - b * 128)
                pp, rpp = bank()
                for kc in range(16):
                    P.op("pe", lambda e, kc=kc, pp=pp, b=b, bs=bs, buf=buf: e.matmul(
                        pp[0:bs, 0:256], k.hT[:, kc, c0 + b * 128:c0 + b * 128 + bs], buf[:, kc * 256: kc * 256 + 256],
                        start=(kc == 0), stop=(kc == 15)), [rbuf, k.r_hT[kc]], [rpp])
                P.op("act", lambda e, pp=pp, b=b, bs=bs, s=s: e.activation(out=cv.sz[0:bs, b, s * 256:(s + 1) * 256], in_=pp[0:bs, 0:256],
                                                                           func=AF.Silu), [rpp], [cv.r_sz])
    for s in range(6):
        if pre and s == 5:
            continue
        buf, rbuf = load_slab(k, "in0", 9 + s)
        for ch in range(2):
            c12 = 2 * s + ch
            fm_group(buf, rbuf, ch * 128, lambda pp, rpp, c12=c12: P.op(
                "act", lambda e: e.activation(out=cv.xbcT[:, c12, 3:3 + Tn], in_=pp[:, 0:Tn], func=AF.Copy), [rpp], [cv.r_xbcT]))
    buf, rbuf = load_slab(k, "in0", 15)
    for b in range(nblk):
        bs = min(128, Tn - b * 128)
        pp, rpp = bank()
        for kc in range(16):
            P.op("pe", lambda e, kc=kc, pp=pp, b=b, bs=bs, buf=buf: e.matmul(pp[0:bs, 0:16], k.hT[:, kc, c0 + b * 128:c0 + b * 128 + bs],
                                                                   buf[:, kc * 256: kc * 256 + 16], start=(kc == 0), stop=(kc == 15)),
                 [rbuf, k.r_hT[kc]], [rpp])
        P.op("dve", lambda e, pp=pp, b=b, bs=bs: e.tensor_tensor(out=cv.dtraw[0:bs, b, :], in0=pp[0:bs, 0:16], in1=k.dtb[0:bs, :],
                                                                 op=ALU.add), [rpp, rc], [cv.r_dtraw])
    chk(k, 1)
    if not pre:
        chkf(k, 1)
        attention(k, Tn, Lc, nch, first_main)
        chkf(k, 4)
    for b in range(nblk):
        bs = min(128, Tn - b * 128)
        ssd_block(k, b, bs, Lc, pre)
        chk(k, 2 if pre else 6)
    if k.dbg and st.get("kv_out") is not None and c0 + Tn >= st["Tn"] and Tn == TM:
        O_ = k.O
        P.op("pool", lambda e: e.dma_start(out=full(O_["d_attnT"]), in_=full(cv.attnT)), [cv.r_attnT], [], dma=True)
        P.op("pool", lambda e: e.dma_start(out=full(O_["d_yT"]), in_=full(cv.yT)), [cv.r_yT], [], dma=True)
        P.op("pool", lambda e: e.dma_start(out=full(O_["d_KT"]), in_=full(k.KT)), [k.r_KT], [], dma=True)
        P.op("pool", lambda e: e.dma_start(out=full(O_["d_QrT"]), in_=full(cv.QrT)), [cv.r_QrT], [], dma=True)
    if Tn >= 128:
        P.op("pool", lambda e: e.tensor_copy(out=k.KT[:, 0:128], in_=k.KT[:, Tn:Tn + 128]), [k.r_KT], [k.r_KT])
        for j in range(2):
            P.op("pool", lambda e, j=j: e.tensor_copy(out=k.Va[j][:, 0:2, :], in_=k.Va[j][:, nch:nch + 2, :]), [k.r_Va], [k.r_Va])
    P.op("pool", lambda e: e.tensor_copy(out=full(k.cvhalo), in_=cv.xbcT[:, :, Tn:Tn + 3]), [cv.r_xbcT], [k.r_cvhalo])
    chk(k, 3)
    if pre:
        return
    for s in range(8):
        buf, rbuf = load_slab(k, "out0", s)
        for ch in range(2):
            dc = 2 * s + ch
            pp, rpp = k.ps[4 + dc % 2], k.r_ps[4 + dc % 2]
            for kc in range(16):
                rhs = cv.attnT[:, kc, 0:Tn] if kc < 8 else cv.yT[:, kc - 8, 0:Tn]
                rr = cv.r_attnT if kc < 8 else cv.r_yT
                P.op("pe", lambda e, kc=kc, pp=pp, rhs=rhs, ch=ch, buf=buf: e.matmul(
                    pp[:, 0:Tn], buf[:, kc * 256 + ch * 128: kc * 256 + ch * 128 + 128], rhs, start=(kc == 0), stop=(kc == 15)),
                    [rbuf, rr], [rpp])
            P.op("dve", lambda e, dc=dc, pp=pp: e.scalar_tensor_tensor(out=k.xT[:, dc, c0:c0 + Tn], in0=pp[:, 0:Tn],
                                                                       scalar=k.G[:, li, dc, seq:seq + 1], in1=k.xT[:, dc, c0:c0 + Tn],
                                                                       op0=ALU.mult, op1=ALU.add),
                 [rpp, k.r_xT[dc], rc], [k.r_xT[dc]])


def attention(k, Tn, Lc, nch, first_main):
    P = k.P
    rc = k.r_const
    cv = carve(k)
    NQ = 8 * Lc
    for c in range(nch):
        t0 = c * Lc
        pts = []
        n = 0
        for kv in range(2):
            for m in (2, 1, 0):
                slot = c + 2 - m
                nk = Lc if m == 0 else 64
                pp, rpp = k.ps[n % 2], k.r_ps[n % 2]
                P.op("pe", lambda e, pp=pp, kv=kv, slot=slot, nk=nk, t0=t0: e.matmul(
                    pp[0:nk, 0:NQ], k.KT[kv * 64:(kv + 1) * 64, slot * 64: slot * 64 + nk],
                    cv.QrT[kv * 64:(kv + 1) * 64, :, t0:t0 + Lc], start=True, stop=True), [k.r_KT, cv.r_QrT], [rpp])
                tS, rtS = k.tmp[n % 2], k.r_tmp[n % 2]
                al = k.alibi[0:nk, m * 512:(m + 1) * 512].rearrange("p (g q) -> p g q", g=8)[:, :, 0:Lc]
                tSv = tS[0:nk, 0:NQ].rearrange("p (g q) -> p g q", g=8)
                ppv = pp[0:nk, 0:NQ].rearrange("p (g q) -> p g q", g=8)
                if kv == 0:
                    P.op("dve", lambda e, tSv=tSv, ppv=ppv, al=al: e.tensor_tensor(out=tSv, in0=ppv, in1=al, op=ALU.add),
                         [rpp, rc], [rtS])
                else:
                    P.op("dve", lambda e, tSv=tSv, ppv=ppv, al=al: e.scalar_tensor_tensor(out=tSv, in0=al, scalar=0.0625, in1=ppv,
                                                                                          op0=ALU.mult, op1=ALU.add),
                         [rpp, rc], [rtS])
                pt, rpt = k.PT[n], k.r_PT[n]
                masked = first_main and slot < 2
                if masked:
                    P.op("act", lambda e, pt=pt, tS=tS, nk=nk: e.activation(out=pt[0:nk, 0:NQ], in_=tS[0:nk, 0:NQ], func=AF.Exp,
                                                                           bias=k.hm[0:nk, 0:1], scale=1.0), [rtS, rc], [rpt])
                else:
                    P.op("act", lambda e, pt=pt, tS=tS, nk=nk: e.activation(out=pt[0:nk, 0:NQ], in_=tS[0:nk, 0:NQ], func=AF.Exp),
                         [rtS], [rpt])
                pts.append((pt, rpt, kv, slot, nk))
                n += 1
        chkf(k, 2)
        po, rpo = k.ps[2], k.r_ps[2]
        pd, rpd = k.ps[3], k.r_ps[3]
        for i, (pt, rpt, kv, slot, nk) in enumerate(pts):
            P.op("pe", lambda e, pt=pt, kv=kv, slot=slot, nk=nk, i=i: e.matmul(po[:, 0:NQ], k.Va[kv][0:nk, slot, :], pt[0:nk, 0:NQ],
                                                                              start=(i == 0), stop=(i == 5)), [rpt, k.r_Va], [rpo])
        for i, (pt, rpt, kv, slot, nk) in enumerate(pts):
            P.op("pe", lambda e, pt=pt, kv=kv, nk=nk, i=i: e.matmul(pd[:, 0:NQ], k.onesk[kv][0:nk, :], pt[0:nk, 0:NQ],
                                                                   start=(i == 0), stop=(i == 5)), [rpt, rc], [rpd])
        chkf(k, 3)
        den, rden = k.sgt[0], k.r_sgt[0]
        denv = den[:, 0:NQ].rearrange("p (g q) -> p g q", g=8)
        P.op("dve", lambda e, denv=denv: e.tensor_tensor(out=denv, in0=pd[:, 0:NQ].rearrange("p (g q) -> p g q", g=8),
                                                         in1=k.esink[:, :].unsqueeze(2).to_broadcast([128, 8, Lc]), op=ALU.add),
             [rpd, rc], [rden])
        chkf(k, 3.3)
        P.op("dve", lambda e, den=den: e.reciprocal(out=den[:, 0:NQ], in_=den[:, 0:NQ]), [rden], [rden])
        chkf(k, 3.6)
        P.op("dve", lambda e, denv=denv, t0=t0: e.tensor_tensor(out=cv.attnT[:, :, t0:t0 + Lc],
                                                                in0=po[:, 0:NQ].rearrange("p (g q) -> p g q", g=8), in1=denv, op=ALU.mult),
             [rpo, rden], [cv.r_attnT])
        chkf(k, 3.9)


def ssd_block(k, b, bs, Lc, pre):
    P = k.P
    rc = k.r_const
    cv = carve(k)
    tb = b * 128
    W = k.wk
    LE, SU, BLK, SEL0, SEL1 = (k.cmask[:, i, :] for i in range(5))
    nchb = max(1, bs // 64)
    u = cv.dtraw[0:bs, b, :]
    P.op("act", lambda e: e.activation(out=W.a16[0:bs, :], in_=u, func=AF.Abs), [cv.r_dtraw], [W.r_a16])
    P.op("act", lambda e: e.activation(out=W.a16[0:bs, :], in_=W.a16[0:bs, :], func=AF.Exp, scale=-1.0), [W.r_a16], [W.r_a16])
    P.op("act", lambda e: e.activation(out=W.a16[0:bs, :], in_=W.a16[0:bs, :], func=AF.Ln, bias=1.0, scale=1.0), [W.r_a16], [W.r_a16])
    P.op("dve", lambda e: e.scalar_tensor_tensor(out=W.dt[0:bs, :], in0=u, scalar=0.0, in1=W.a16[0:bs, :], op0=ALU.max, op1=ALU.add),
         [cv.r_dtraw, W.r_a16], [W.r_dt])
    P.op("dve", lambda e: e.tensor_tensor(out=W.dA[0:bs, :], in0=W.dt[0:bs, :], in1=k.aneg[0:bs, :], op=ALU.mult), [W.r_dt, rc], [W.r_dA])
    pc, rpc = k.ps[0], k.r_ps[0]
    P.op("pe", lambda e: e.matmul(pc[0:bs, 0:16], LE[0:bs, 0:bs], W.dA[0:bs, :], start=True, stop=True), [W.r_dA, rc], [rpc])
    P.op("pe", lambda e: e.matmul(pc[0:bs, 16:32], BLK[0:bs, 0:bs], W.dA[0:bs, :], start=True, stop=True), [W.r_dA, rc], [rpc])
    P.op("pe", lambda e: e.matmul(pc[:, 32:48], SEL0[0:bs, :], W.dA[0:bs, :], start=True, stop=True), [W.r_dA, rc], [rpc])
    if nchb == 2:
        P.op("pe", lambda e: e.matmul(pc[:, 48:64], SEL1[0:bs, :], W.dA[0:bs, :], start=True, stop=True), [W.r_dA, rc], [rpc])
    P.op("act", lambda e: e.activation(out=W.s64[0:bs, 0:16], in_=pc[0:bs, 0:16], func=AF.Copy), [rpc], [W.r_s64])
    P.op("act", lambda e: e.activation(out=W.s64[0:bs, 16:32], in_=pc[0:bs, 0:16], func=AF.Exp), [rpc], [W.r_s64])
    P.op("dve", lambda e: e.tensor_tensor(out=W.s64[0:bs, 32:48], in0=pc[0:bs, 16:32], in1=W.s64[0:bs, 0:16], op=ALU.subtract),
         [rpc, W.r_s64], [W.r_s64])
    P.op("act", lambda e: e.activation(out=W.s64[0:bs, 32:48], in_=W.s64[0:bs, 32:48], func=AF.Exp), [W.r_s64], [W.r_s64])
    P.op("dve", lambda e: e.tensor_tensor(out=W.s64[0:bs, 32:48], in0=W.s64[0:bs, 32:48], in1=W.dt[0:bs, :], op=ALU.mult),
         [W.r_s64, W.r_dt], [W.r_s64])
    P.op("act", lambda e: e.activation(out=W.cdb[:, 0:16 * nchb], in_=pc[:, 32:32 + 16 * nchb], func=AF.Exp), [rpc], [W.r_cdb])
    dt_b = W.dt[0:bs, :]
    dtd_b = W.s64[0:bs, 32:48]
    expcum = W.s64[0:bs, 16:32]
    if not pre:
        chkf(k, 4.1)
    for c12 in range(12):
        if pre and c12 >= 10:
            continue
        t, rt_ = k.tmp[c12 % 2], k.r_tmp[c12 % 2]
        xin = cv.xbcT
        P.op("dve", lambda e, t=t, c12=c12: e.tensor_scalar(out=t[:, 0:bs], in0=xin[:, c12, tb:tb + bs], scalar1=k.cwT[:, c12, 0:1],
                                                            scalar2=None, op0=ALU.mult), [cv.r_xbcT, rc], [rt_])
        for i in (1, 2, 3):
            P.op("dve", lambda e, t=t, c12=c12, i=i: e.scalar_tensor_tensor(out=t[:, 0:bs], in0=xin[:, c12, tb + i:tb + i + bs],
                                                                            scalar=k.cwT[:, c12, i:i + 1], in1=t[:, 0:bs],
                                                                            op0=ALU.mult, op1=ALU.add), [cv.r_xbcT, rt_, rc], [rt_])
        if c12 < 8:
            P.op("act", lambda e, t=t, c12=c12: e.activation(out=W.xc[:, c12, 0:bs], in_=t[:, 0:bs], func=AF.Silu,
                                                             bias=k.cbT[:, c12:c12 + 1], scale=1.0), [rt_, rc], [W.r_xc])
        elif c12 < 10:
            P.op("act", lambda e, t=t, c12=c12: e.activation(out=W.BT[:, c12 - 8, 0:bs], in_=t[:, 0:bs], func=AF.Silu,
                                                             bias=k.cbT[:, c12:c12 + 1], scale=1.0), [rt_, rc], [W.r_BT])
        else:
            P.op("act", lambda e, t=t, c12=c12: e.activation(out=W.CT[:, c12 - 10, 0:bs], in_=t[:, 0:bs], func=AF.Silu,
                                                             bias=k.cbT[:, c12:c12 + 1], scale=1.0), [rt_, rc], [W.r_CT])
    if not pre:
        chkf(k, 4.2)
    for half in range(2):
        pt, rpt = k.ps[2 + half], k.r_ps[2 + half]
        for i in range(4):
            c = half * 4 + i
            P.op("pe", lambda e, pt=pt, c=c, i=i: e.transpose(pt[0:bs, i * 128:(i + 1) * 128], W.xc[:, c, 0:bs], k.ident[:, :]),
                 [W.r_xc, rc], [rpt])
        ptv = pt[0:bs, :].rearrange("p (j q) -> p j q", q=64)
        hs_ = slice(half * 8, half * 8 + 8)
        cs_ = slice(half * 512, half * 512 + 512)
        import os
        SK = os.environ.get("KSKIP", "")
        if not pre:
            if "xdt" not in SK:
                P.op("dve", lambda e, ptv=ptv, hs_=hs_, cs_=cs_: e.tensor_tensor(
                    out=W.xdt[0:bs, cs_].rearrange("p (j q) -> p j q", q=64), in0=ptv,
                    in1=dt_b[:, hs_].unsqueeze(2).to_broadcast([bs, 8, 64]), op=ALU.mult), [rpt, W.r_dt], [W.r_xdt])
            if "xs" not in SK:
                P.op("dve", lambda e, pt=pt, cs_=cs_: e.tensor_tensor(out=W.xs[0:bs, cs_], in0=pt[0:bs, :], in1=k.DT[0:bs, cs_],
                                                                      op=ALU.mult), [rpt, rc], [W.r_xs])
        P.op("dve", lambda e, ptv=ptv, hs_=hs_, cs_=cs_: e.tensor_tensor(
            out=W.xdtd[0:bs, cs_].rearrange("p (j q) -> p j q", q=64), in0=ptv,
            in1=dtd_b[:, hs_].unsqueeze(2).to_broadcast([bs, 8, 64]), op=ALU.mult), [rpt, W.r_s64], [W.r_xdtd])
    if not pre:
        chkf(k, 4.3)
    pbt, rpbt = k.ps[1], k.r_ps[1]
    pbtb = pbt[:, :].bitcast(BF16)
    for g in range(2):
        P.op("pe", lambda e, g=g: e.transpose(pbtb[0:bs, g * 128:(g + 1) * 128], W.BT[:, g, 0:bs], k.identb[:, :]), [W.r_BT, rc], [rpbt])
    P.op("act", lambda e: e.activation(out=W.Btok[0:bs, :], in_=pbtb[0:bs, 0:256], func=AF.Copy), [rpbt], [W.r_Btok])
    if not pre:
        chkf(k, 4.4)
    if not pre:
        pcb, rpcb = k.ps[1], k.r_ps[1]
        for g in range(2):
            P.op("pe", lambda e, g=g: e.matmul(pcb[0:bs, 256 + g * 128: 256 + g * 128 + bs], W.BT[:, g, 0:bs], W.CT[:, g, 0:bs],
                                               start=True, stop=True), [W.r_BT, W.r_CT], [rpcb])
        chkf(k, 4.5)
        for g in range(2):
            P.op("dve", lambda e, g=g: e.tensor_tensor(out=W.CBm[0:bs, g, 0:bs], in0=pcb[0:bs, 256 + g * 128: 256 + g * 128 + bs],
                                                       in1=LE[0:bs, 0:bs], op=ALU.mult), [rpcb, rc], [W.r_CBm])
        chkf(k, 5.1)
        for g in range(2):
            P.op("dve", lambda e, g=g: e.tensor_tensor(
                out=cv.R[0:bs, :, 0:bs], in0=W.dA[0:bs, g * 8:(g + 1) * 8].unsqueeze(2).to_broadcast([bs, 8, bs]),
                in1=LE[0:bs, 0:bs].unsqueeze(1).to_broadcast([bs, 8, bs]), op=ALU.mult), [W.r_dA, rc], [cv.r_R])
            psg = [k.ps[4], k.ps[5]]
            rpsg = [k.r_ps[4], k.r_ps[5]]
            hpb = max(1, 512 // bs)
            for j0 in range(0, 8, hpb):
                bk = (j0 // hpb) % 2
                P.op("pe", lambda e, j0=j0, bk=bk: e.matmul(
                    psg[bk][0:bs, 0:hpb * bs].rearrange("p (j q) -> p j q", q=bs) if False else psg[bk][0:bs, 0:min(8, hpb) * bs],
                    SU[0:bs, 0:bs], cv.R[0:bs, j0:j0 + min(8, hpb), 0:bs], start=True, stop=True), [cv.r_R, rc], [rpsg[bk]])
                nh = min(8, hpb)
                P.op("act", lambda e, j0=j0, bk=bk, nh=nh: e.activation(
                    out=cv.Lh[0:bs, j0:j0 + nh, 0:bs], in_=psg[bk][0:bs, 0:nh * bs].rearrange("p (j q) -> p j q", q=bs), func=AF.Exp),
                    [rpsg[bk]], [cv.r_Lh])
            P.op("dve", lambda e, g=g: e.tensor_tensor(
                out=W.MT[0:bs, g * 8:(g + 1) * 8, 0:bs], in0=cv.Lh[0:bs, :, 0:bs],
                in1=W.CBm[0:bs, g, 0:bs].unsqueeze(1).to_broadcast([bs, 8, bs]), op=ALU.mult), [cv.r_Lh, W.r_CBm], [W.r_MT])
        chkf(k, 5.2)
        py = [k.ps[6], k.ps[7]]
        rpy = [k.r_ps[6], k.r_ps[7]]
        for j in range(16):
            bk = j // 8
            P.op("pe", lambda e, j=j, bk=bk: e.matmul(py[bk][0:bs, (j % 8) * 64:(j % 8) * 64 + 64], W.MT[0:bs, j, 0:bs],
                                                      W.xdt[0:bs, j * 64:(j + 1) * 64], start=True, stop=True),
                 [W.r_MT, W.r_xdt], [rpy[bk]])
        chkf(k, 5.3)
        if nchb == 2:
            P.op("pool", lambda e: e.tensor_copy(out=W.CT0[:, :, 0:64], in_=W.CT[:, :, 0:64]), [W.r_CT], [W.r_CT0])
            P.op("pool", lambda e: e.tensor_copy(out=W.CT1[:, :, 64:128], in_=W.CT[:, :, 64:128]), [W.r_CT], [W.r_CT1])
    po = [k.ps[4], k.ps[5]]
    rpo = [k.r_ps[4], k.r_ps[5]]
    for cc in range(nchb):
        if not pre:
            lhs = (W.CT if nchb == 1 else (W.CT0 if cc == 0 else W.CT1))
            rl = (W.r_CT if nchb == 1 else (W.r_CT0 if cc == 0 else W.r_CT1))
            for g in range(2):
                P.op("pe", lambda e, g=g, cc=cc, lhs=lhs: e.matmul(po[g][0:bs, :], lhs[:, g, 0:bs], k.hb[cc][:, g * 512:(g + 1) * 512],
                                                                   start=(cc == 0), stop=(cc == nchb - 1)), [rl, k.r_hb[cc]], [rpo[g]])
        rows = slice(cc * 64, cc * 64 + min(64, bs))
        pst_ = [k.ps[2], k.ps[3]]
        rpst = [k.r_ps[2], k.r_ps[3]]
        for g in range(2):
            P.op("pe", lambda e, g=g, rows=rows: e.matmul(pst_[g][:, :], W.Btok[rows, g * 128:(g + 1) * 128],
                                                          W.xdtd[rows, g * 512:(g + 1) * 512], start=True, stop=True),
                 [W.r_Btok, W.r_xdtd], [rpst[g]])
        for g in range(2):
            hv = k.hs[:, g * 512:(g + 1) * 512].rearrange("p (j q) -> p j q", q=64)
            P.op("dve", lambda e, g=g, hv=hv, cc=cc: e.tensor_tensor(
                out=hv, in0=hv, in1=W.cdb[:, cc * 16 + g * 8: cc * 16 + g * 8 + 8].unsqueeze(2).to_broadcast([128, 8, 64]),
                op=ALU.mult), [k.r_hs, W.r_cdb], [k.r_hs])
            P.op("dve", lambda e, g=g: e.tensor_tensor(out=k.hs[:, g * 512:(g + 1) * 512], in0=k.hs[:, g * 512:(g + 1) * 512],
                                                       in1=pst_[g][:, :], op=ALU.add), [k.r_hs, rpst[g]], [k.r_hs])
        nxt = (cc + 1) % 2 if nchb == 2 else 0
        P.op("act", lambda e, nxt=nxt: e.activation(out=k.hb[nxt][:, :], in_=k.hs[:, :], func=AF.Copy), [k.r_hs], [k.r_hb[nxt]])
    if pre:
        return
    chkf(k, 5.4)
    P.op("pool", lambda e: e.memset(W.ss2[0:bs, :], 0.0), [], [W.r_ss2])
    for g in range(2):
        cs_ = slice(g * 512, g * 512 + 512)
        yv = W.y[0:bs, cs_].rearrange("p (j q) -> p j q", q=64)
        P.op("dve", lambda e, g=g, yv=yv: e.tensor_tensor(out=yv, in0=po[g][0:bs, :].rearrange("p (j q) -> p j q", q=64),
                                                          in1=expcum[:, g * 8:(g + 1) * 8].unsqueeze(2).to_broadcast([bs, 8, 64]),
                                                          op=ALU.mult), [rpo[g], W.r_s64], [W.r_y])
        P.op("dve", lambda e, g=g, cs_=cs_: e.tensor_tensor(out=W.y[0:bs, cs_], in0=W.y[0:bs, cs_], in1=py[g][0:bs, :], op=ALU.add),
             [W.r_y, rpy[g]], [W.r_y])
        P.op("dve", lambda e, cs_=cs_: e.tensor_tensor(out=W.y[0:bs, cs_], in0=W.y[0:bs, cs_], in1=W.xs[0:bs, cs_], op=ALU.add),
             [W.r_y, W.r_xs], [W.r_y])
        P.op("dve", lambda e, cs_=cs_: e.tensor_tensor(out=W.y[0:bs, cs_], in0=W.y[0:bs, cs_], in1=cv.sz[0:bs, b, cs_], op=ALU.mult),
             [W.r_y, cv.r_sz], [W.r_y])
        P.op("act", lambda e, g=g, cs_=cs_: e.activation(out=W.xs[0:bs, cs_], in_=W.y[0:bs, cs_], func=AF.Square,
                                                         accum_out=W.ss2[0:bs, g:g + 1]), [W.r_y, W.r_xs], [W.r_xs, W.r_ss2])
    chkf(k, 5.5)
    P.op("act", lambda e: e.activation(out=W.ss2[0:bs, :], in_=W.ss2[0:bs, :], func=AF.Sqrt, bias=EPS, scale=1.0 / 512.0),
         [W.r_ss2], [W.r_ss2])
    P.op("dve", lambda e: e.reciprocal(out=W.ss2[0:bs, :], in_=W.ss2[0:bs, :]), [W.r_ss2], [W.r_ss2])
    for g in range(2):
        cs_ = slice(g * 512, g * 512 + 512)
        P.op("dve", lambda e, g=g, cs_=cs_: e.scalar_tensor_tensor(out=W.yn[0:bs, cs_], in0=W.y[0:bs, cs_], scalar=W.ss2[0:bs, g:g + 1],
                                                                   in1=k.NG[0:bs, cs_], op0=ALU.mult, op1=ALU.mult),
             [W.r_y, W.r_ss2, rc], [W.r_yn])
    chkf(k, 5.6)
    for half in range(2):
        pt, rpt = k.ps[2 + half], k.r_ps[2 + half]
        ptb = pt[:, :].bitcast(BF16)
        for i in range(4):
            c = half * 4 + i
            P.op("pe", lambda e, ptb=ptb, c=c, i=i: e.transpose(ptb[:, i * 128:i * 128 + bs], W.yn[0:bs, c * 128:(c + 1) * 128],
                                                                k.identb[0:bs, 0:bs]), [W.r_yn, rc], [rpt])
        P.op("act", lambda e, ptb=ptb, half=half: e.activation(
            out=cv.yT[:, half * 4:half * 4 + 4, tb:tb + bs], in_=ptb[:, 0:512].rearrange("p (c t) -> p c t", c=4)[:, :, 0:bs],
            func=AF.Copy), [rpt], [cv.r_yT])


def mixer1(k, seq, Tn):
    P = k.P
    rc = k.r_const
    li = 4
    vT = k.big[:, 0:16 * TT].rearrange("p (c t) -> p c t", c=16)
    r_vT = k.r_act[0:16]
    W = k.wk
    for c in range(16):
        sb_gb, r_gb = load_slab(k, "in1", c // 2) if c % 2 == 0 else (k._gb, k._rgb)
        sb_gc, r_gc = load_slab(k, "in1", 8 + c // 2) if c % 2 == 0 else (k._gc, k._rgc)
        sb_xi, r_xi = load_slab(k, "in1", 16 + c // 2) if c % 2 == 0 else (k._xi, k._rxi)
        k._gb, k._rgb, k._gc, k._rgc, k._xi, k._rxi = sb_gb, r_gb, sb_gc, r_gc, sb_xi, r_xi
        ch = c % 2
        outs = []
        for (buf, rbuf, bk) in ((sb_gc, r_gc, 0), (sb_xi, r_xi, 1), (sb_gb, r_gb, 2)):
            pp, rpp = k.ps[bk], k.r_ps[bk]
            for kc in range(16):
                P.op("pe", lambda e, kc=kc, pp=pp, buf=buf, ch=ch: e.matmul(
                    pp[:, 0:Tn], buf[:, kc * 256 + ch * 128: kc * 256 + ch * 128 + 128], k.hT[:, kc, 0:Tn],
                    start=(kc == 0), stop=(kc == 15)), [rbuf, k.r_hT[kc]], [rpp])
            outs.append((pp, rpp))
        (pgc, rpgc), (pxi, rpxi), (pgb, rpgb) = outs
        pt, rpt_ = W.ptmp, W.r_ptmp
        P.op("act", lambda e, pgc=pgc: e.activation(out=k.sgt[0][:, 0:Tn], in_=pgc[:, 0:Tn], func=AF.Copy), [rpgc], [k.r_sgt[0]])
        P.op("pool", lambda e, c=c: e.tensor_copy(out=pt[:, 0:2], in_=k.phalo[:, c, :]), [k.r_phalo], [rpt_])
        P.op("dve", lambda e, pxi=pxi: e.tensor_tensor(out=pt[:, 2:2 + Tn], in0=k.sgt[0][:, 0:Tn], in1=pxi[:, 0:Tn], op=ALU.mult),
             [k.r_sgt[0], rpxi], [rpt_])
        P.op("pool", lambda e, c=c: e.tensor_copy(out=k.phalo[:, c, :], in_=pt[:, Tn:Tn + 2]), [rpt_], [k.r_phalo])
        u, ru = k.tmp[c % 2], k.r_tmp[c % 2]
        P.op("dve", lambda e, u=u, c=c: e.tensor_scalar(out=u[:, 0:Tn], in0=pt[:, 0:Tn], scalar1=k.swT[:, c, 0:1], scalar2=None,
                                                        op0=ALU.mult), [rpt_, rc], [ru])
        for i in (1, 2):
            P.op("dve", lambda e, u=u, c=c, i=i: e.scalar_tensor_tensor(out=u[:, 0:Tn], in0=pt[:, i:i + Tn], scalar=k.swT[:, c, i:i + 1],
                                                                        in1=u[:, 0:Tn], op0=ALU.mult, op1=ALU.add), [rpt_, ru, rc], [ru])
        P.op("dve", lambda e, u=u, c=c, pgb=pgb: e.tensor_tensor(out=vT[:, c, 0:Tn], in0=u[:, 0:Tn], in1=pgb[:, 0:Tn], op=ALU.mult),
             [ru, rpgb], [r_vT[c]])
    for s in range(8):
        buf, rbuf = load_slab(k, "out1", s)
        for ch in range(2):
            dc = 2 * s + ch
            pp, rpp = k.ps[4 + dc % 2], k.r_ps[4 + dc % 2]
            for kc in range(16):
                P.op("pe", lambda e, kc=kc, pp=pp, ch=ch, buf=buf: e.matmul(
                    pp[:, 0:Tn], buf[:, kc * 256 + ch * 128: kc * 256 + ch * 128 + 128], vT[:, kc, 0:Tn], start=(kc == 0), stop=(kc == 15)),
                    [rbuf, r_vT[kc]], [rpp])
            P.op("dve", lambda e, dc=dc, pp=pp: e.scalar_tensor_tensor(out=k.xT[:, dc, 0:Tn], in0=pp[:, 0:Tn],
                                                                       scalar=k.G[:, li, dc, seq:seq + 1], in1=k.xT[:, dc, 0:Tn],
                                                                       op0=ALU.mult, op1=ALU.add),
                 [rpp, k.r_xT[dc], rc], [k.r_xT[dc]])


def alloc_work(k):
    sb = lambda name, shape, dt: k.st.enter_context(k.nc.sbuf_tensor("s_" + name, list(shape), dt))
    W = K()
    for nm, shape, dt in (("a16", [128, 16], F32), ("dt", [128, 16], F32), ("dA", [128, 16], F32), ("s64", [128, 64], F32),
                          ("cdb", [128, 32], F32), ("xc", [128, 8, 128], F32), ("BT", [128, 2, 128], BF16), ("CT", [128, 2, 128], BF16),
                          ("CT0", [128, 2, 128], BF16), ("CT1", [128, 2, 128], BF16), ("xdt", [128, 1024], BF16),
                          ("xdtd", [128, 1024], BF16), ("xs", [128, 1024], F32), ("Btok", [128, 256], BF16),
                          ("CBm", [128, 2, 128], F32),
                          ("MT", [128, 16, 128], BF16), ("y", [128, 1024], F32), ("yn", [128, 1024], BF16), ("ss2", [128, 2], F32),
                          ("ptmp", [128, TT + 4], F32)):
        setattr(W, nm, sb("w_" + nm, shape, dt))
        setattr(W, "r_" + nm, Res("w_" + nm))
    k.wk = W
    k.PT = [sb("PT%d" % j, [64, 512], BF16) for j in range(6)]
    k.r_PT = [Res("PT%d" % j) for j in range(6)]
    P = k.P
    P.op("pool", lambda e: e.memset(full(W.CT0), 0.0), [], [W.r_CT0])
    P.op("pool", lambda e: e.memset(full(W.CT1), 0.0), [], [W.r_CT1])
    P.op("pool", lambda e: e.memset(full(carve(k).xbcT), 0.0), [], [carve(k).r_xbcT])


def tile_pass(k, kind, src, Tn, seq, dst=None, first_main=False):
    P = k.P
    P.op("sp", lambda e: e.dma_start(out=k.xT[:, :, 0:Tn], in_=src), [], k.r_xT, dma=True)
    L0 = 0 in k.layers
    L1 = 1 in k.layers
    if L0:
        norm_mod(k, 0, seq, Tn)
        ffn(k, 0, 0, seq, Tn)
        norm_mod(k, 1, seq, Tn)
        c0 = 0
        while c0 < Tn:
            tm = min(TM, Tn - c0)
            mixer0(k, seq, tm, c0, "pre" if kind == "pre" else "full", first_main and c0 == 0)
            c0 += tm
        if kind == "pre":
            return
        norm_mod(k, 2, seq, Tn)
        ffn(k, 0, 1, seq, Tn)
    if kind == "pre":
        return
    if L1:
        norm_mod(k, 3, seq, Tn)
        ffn(k, 1, 0, seq, Tn)
        norm_mod(k, 4, seq, Tn)
        mixer1(k, seq, Tn)
        if kind != "halo":
            norm_mod(k, 5, seq, Tn)
            ffn(k, 1, 1, seq, Tn)
    if kind == "halo":
        return
    norm_mod(k, 0, seq, Tn, final=True)
    P.op("pool", lambda e: e.dma_start(out=dst, in_=k.xT[:, :, 0:Tn]), k.r_xT, [], dma=True)


def scale_by_flag(k, ap, reads_writes):
    k.P.op("dve", lambda e: e.tensor_scalar(out=ap, in0=ap, scalar1=k.flag[:, 0:1], scalar2=None, op0=ALU.mult),
           list(reads_writes) + [k.r_const], list(reads_writes))


def main_phase(k):
    P, I, O = k.P, k.I, k.O
    cv = carve(k)
    alloc_work(k)
    k.st_m0 = {}
    seqh = k.seqh
    npre_tok = seqh - 128
    make_bg_jobs(k)
    k.bg_on = len(k.bg_jobs) > 0
    k.bg_slot = 0
    k.bg_slots = max(1, int(((npre_tok + TT - 1) // TT) * 38 * 0.95))
    t = 0
    while t < npre_tok:
        Tn = min(TT, npre_tok - t)
        tile_pass(k, "pre", I["xqT"][:, :, t:t + Tn], Tn, 0)
        t += Tn
    k.bg_on = False
    bg_step(k, len(k.bg_jobs))
    tile_pass(k, "halo", I["xqT"][:, :, seqh - 128:seqh], 128, 0)
    scale_by_flag(k, k.hs[:, :], [k.r_hs])
    P.op("act", lambda e: e.activation(out=k.hb[0][:, :], in_=k.hs[:, :], func=AF.Copy), [k.r_hs], [k.r_hb[0]])
    scale_by_flag(k, full(k.cvhalo), [k.r_cvhalo])
    scale_by_flag(k, full(k.phalo), [k.r_phalo])
    for ti in range(k.ntile):
        last = ti == k.ntile - 1
        k.st_m0 = {"kv_out": (O["kp"], O["vp"]), "Tn": TT} if last else {}
        tile_pass(k, "main", I["xpT"][:, :, ti * TT:(ti + 1) * TT], TT, 0, dst=O["ypT"][:, :, ti * TT:(ti + 1) * TT],
                  first_main=(ti == 0))
    P.op("pool", lambda e: e.dma_start(out=O["hp"][:, :], in_=k.hs[:, :]), [k.r_hs], [], dma=True)
    P.op("pool", lambda e: e.dma_start(out=full(O["cvp"]), in_=full(k.cvhalo)), [k.r_cvhalo], [], dma=True)
    P.op("pool", lambda e: e.dma_start(out=full(O["scp"]), in_=full(k.phalo)), [k.r_phalo], [], dma=True)
    P.op("sp", lambda e: e.dma_start(out=k.hs[:, :], in_=I["st_hT"][:, :]), [], [k.r_hs], dma=True)
    P.op("act", lambda e: e.activation(out=k.hb[0][:, :], in_=k.hs[:, :], func=AF.Copy), [k.r_hs], [k.r_hb[0]])
    P.op("sp", lambda e: e.dma_start(out=full(k.cvhalo), in_=full(I["st_cvT"])), [], [k.r_cvhalo], dma=True)
    P.op("sp", lambda e: e.dma_start(out=full(k.phalo), in_=full(I["st_scT"])), [], [k.r_phalo], dma=True)
    P.op("sp", lambda e: e.dma_start(out=k.sgt[1][:, 0:128], in_=I["ckT"][:, :]), [], [k.r_sgt[1]], dma=True)
    P.op("dve", lambda e: e.tensor_copy(out=k.KT[:, 0:128], in_=k.sgt[1][:, 0:128]), [k.r_sgt[1]], [k.r_KT])
    P.op("sp", lambda e: e.dma_start(out=k.sgt[0][0:64, 0:256].rearrange("p (c d) -> p c d", c=2), in_=full(I["cvc"])), [],
         [k.r_sgt[0]], dma=True)
    for j in range(2):
        P.op("dve", lambda e, j=j: e.tensor_copy(out=k.Va[j][:, 0:2, j * 64:(j + 1) * 64],
                                                 in_=k.sgt[0][0:64, 0:256].rearrange("p (c d) -> p c d", c=2)[:, :, j * 64:(j + 1) * 64]),
             [k.r_sgt[0]], [k.r_Va])
    k.st_m0 = {"kv_out": (O["ks"], O["vs"]), "Tn": 16}
    tile_pass(k, "sample", I["xsT"][:, :, 0:16], 16, 1, dst=O["ysT"][:, :, 0:16])
    P.op("pool", lambda e: e.dma_start(out=O["hs"][:, :], in_=k.hs[:, :]), [k.r_hs], [], dma=True)
    P.op("pool", lambda e: e.dma_start(out=full(O["cvs"]), in_=full(k.cvhalo)), [k.r_cvhalo], [], dma=True)
    P.op("pool", lambda e: e.dma_start(out=full(O["scs"]), in_=full(k.phalo)), [k.r_phalo], [], dma=True)


def _slabs_k16(W, ncol_slab=256):
    Kd, N = W.shape
    ns = N // ncol_slab
    return np.ascontiguousarray(W.reshape(Kd // 128, 128, ns, ncol_slab).transpose(2, 1, 0, 3)).reshape(ns, 128, -1)


def _slabs_down(Wd):
    a = Wd.reshape(2, 22, 128, 16, 128).transpose(3, 0, 2, 1, 4)
    return np.ascontiguousarray(a).reshape(32, 128, 22 * 128)


def _fm(x):
    T, Dd = x.shape
    return np.ascontiguousarray(x.reshape(T, Dd // 128, 128).transpose(2, 1, 0))


def _unfm(a):
    return np.ascontiguousarray(a.transpose(2, 1, 0)).reshape(a.shape[2], -1)


def _consts():
    f = np.float32
    r = np.arange(128)
    same = (r[:, None] // 64) == (r[None, :] // 64)
    LE = (same & (r[:, None] <= r[None, :])).astype(f)
    SU = (same & (r[:, None] > r[None, :])).astype(f)
    BLK = same.astype(f)
    SEL0 = np.repeat((r < 64).astype(f)[:, None], 128, axis=1)
    SEL1 = np.repeat((r >= 64).astype(f)[:, None], 128, axis=1)
    cmask = np.ascontiguousarray(np.stack([LE, SU, BLK, SEL0, SEL1], axis=1))
    slope0 = (2.0 ** (-8.0 * np.arange(1, 9) / 16.0)).astype(f)
    kk = np.arange(64)[:, None, None]
    qq = np.arange(64)[None, None, :]
    al = np.zeros((64, 3, 8, 64), f)
    for m in range(3):
        dist = np.abs(qq + 64 * m - kk).astype(f)
        al[:, m] = -(slope0[None, :, None] * dist)
    return cmask, np.ascontiguousarray(al.reshape(64, 3 * 512))


def prep_shared(inp, layers=(0, 1), ffn_on=True):
    sh = {}
    f = np.float32
    sh["ident"] = np.eye(128, dtype=f)
    sh["cmask"], sh["alibi"] = _consts()
    sh["norm_gT"] = np.ascontiguousarray(inp["norm_g"].reshape(6, 16, 128).transpose(2, 0, 1))
    sh["fnormT"] = np.ascontiguousarray(inp["final_norm_g"].reshape(16, 128).T)
    sh["b_adaT"] = np.ascontiguousarray(inp["b_ada"].reshape(288, 128).T)
    sh["w_ada"] = np.concatenate([_slabs_k16(inp["w_ada"][l]) for l in range(2)], axis=0)
    rep = lambda v: np.ascontiguousarray(np.broadcast_to(np.asarray(v, f).reshape(1, -1), (128, v.size)))
    sk = inp["attn_sinks"][0]
    sh["sinks2"] = np.ascontiguousarray(np.concatenate([np.broadcast_to(sk[0:8], (64, 8)), np.broadcast_to(sk[8:16], (64, 8))], axis=0))
    sh["dtb_b"] = rep(inp["ssd_dt_bias"][0])
    sh["alog_b"] = rep(inp["ssd_a_log"][0])
    sh["d_b"] = rep(inp["ssd_d"][0])
    sh["ng_b"] = rep(inp["ssd_norm_g"][0])
    sh["conv_wT"] = np.ascontiguousarray(inp["ssd_conv_w"][0].reshape(4, 12, 128).transpose(2, 1, 0))
    sh["conv_bT"] = np.ascontiguousarray(inp["ssd_conv_b"][0].reshape(12, 128).T)
    sh["sconv_wT"] = np.ascontiguousarray(inp["sconv_w"][0].reshape(3, 16, 128).transpose(2, 1, 0))
    W = {}
    for l in range(2):
        for i in range(2):
            if ffn_on and l in layers:
                W["gu%d%d" % (l, i)] = np.concatenate([_slabs_k16(inp["w_ffn_gate"][l, i], 128),
                                                       _slabs_k16(inp["w_ffn_up"][l, i], 128)], axis=0)
                W["dn%d%d" % (l, i)] = _slabs_down(inp["w_ffn_down"][l, i])
    if 0 in layers:
        w0 = inp["w_in_mix0"][0]
        wq = w0[:, 0:1024].reshape(2048, 2, 8, 64).transpose(0, 2, 1, 3).reshape(2048, 1024)
        w0p = np.zeros((2048, 4096), f)
        w0p[:, 0:1024] = wq
        w0p[:, 1024:3856] = w0[:, 1024:3856]
        W["in0"] = _slabs_k16(w0p)
        wo = inp["w_out_mix0"][0]
        woa = wo[0:1024].reshape(2, 8, 64, 2048).transpose(1, 0, 2, 3).reshape(1024, 2048)
        W["out0"] = _slabs_k16(np.concatenate([woa, wo[1024:2048]], axis=0))
    if 1 in layers:
        W["in1"] = _slabs_k16(inp["w_in_mix1"][0])
        W["out1"] = _slabs_k16(inp["w_out_mix1"][0])
    for name, a in W.items():
        sh["w_" + name] = a
    return sh


def prep_core(inp, i, sh, ntile=NTILE):
    f = np.float32
    s, half = i // 2, i % 2
    seqh = ntile * TT
    m = dict(sh)
    m["xpT"] = _fm(inp["x_prompt"][s, half * seqh:(half + 1) * seqh])
    m["xqT"] = _fm(inp["x_prompt"][s, 0:seqh]) if half == 1 else np.zeros((128, 16, seqh), f)
    m["xsT"] = _fm(inp["x_sample"][i])
    m["c2T"] = _fm(np.stack([inp["c_prompt"][s], inp["c_sample"][i]], axis=0))
    m["flag"] = np.full((128, 1), float(half), f)
    m["ckT"] = np.ascontiguousarray(inp["cache_swa_k"][0, i].reshape(128, 128).T)
    m["cvc"] = np.ascontiguousarray(inp["cache_swa_v"][0, i].reshape(2, 64, 128).transpose(1, 0, 2))
    m["st_hT"] = np.ascontiguousarray(inp["state_ssd"][0, i].reshape(1024, 128).T)
    m["st_cvT"] = _fm(inp["state_ssd_conv"][0, i])
    m["st_scT"] = _fm(inp["state_sconv"][0, i])
    return m


def assemble(results, ntile=NTILE, cores=None):
    f = np.float32
    seqh = ntile * TT
    cores = list(range(NCORES)) if cores is None else cores
    yp = np.zeros((4, 2 * seqh, D), f)
    ys = np.zeros((8, 16, D), f)
    kp = np.zeros((1, 4, 128, 2, 64), f)
    vp = np.zeros((1, 4, 128, 2, 64), f)
    hp = np.zeros((1, 4, 16, 64, 128), f)
    cvp = np.zeros((1, 4, 3, 1536), f)
    scp = np.zeros((1, 4, 2, 2048), f)
    ks = np.zeros((1, 8, 16, 2, 64), f)
    vs = np.zeros((1, 8, 16, 2, 64), f)
    hs = np.zeros((1, 8, 16, 64, 128), f)
    cvs = np.zeros((1, 8, 3, 1536), f)
    scs = np.zeros((1, 8, 2, 2048), f)
    for idx, i in enumerate(cores):
        r = results[idx]
        s, half = i // 2, i % 2
        yp[s, half * seqh:(half + 1) * seqh] = _unfm(r["ypT"])
        ys[i] = _unfm(r["ysT"])
        if half == 1:
            kp[0, s] = r["o_kp"].reshape(128, 2, 64)
            vp[0, s] = r["o_vp"].reshape(128, 2, 64)
            hp[0, s] = r["o_hp"].T.reshape(16, 64, 128)
            cvp[0, s] = _unfm(r["o_cvp"])
            scp[0, s] = _unfm(r["o_scp"])
        ks[0, i] = r["o_ks"].reshape(16, 2, 64)
        vs[0, i] = r["o_vs"].reshape(16, 2, 64)
        hs[0, i] = r["o_hs"].T.reshape(16, 64, 128)
        cvs[0, i] = _unfm(r["o_cvs"])
        scs[0, i] = _unfm(r["o_scs"])
    return (yp, ys, kp, vp, hp, cvp, scp, ks, vs, hs, cvs, scs)


_NC_CACHE = {}


def kernel(**inputs):
    inp = {k_: np.asarray(v) for k_, v in inputs.items()}
    if "nc" not in _NC_CACHE:
        _NC_CACHE["nc"] = build()
    nc = _NC_CACHE["nc"]
    sh = prep_shared(inp)
    in_maps = [prep_core(inp, i, sh) for i in range(NCORES)]
    res = run_bass_kernel_spmd(nc, in_maps, core_ids=list(range(NCORES)))
    return assemble(res.results)
```

```python
import numpy as np
from contextlib import ExitStack
import concourse.bass as bass
import concourse.mybir as mybir
from concourse.bass_utils import run_bass_kernel_spmd

F32 = mybir.dt.float32
BF16 = mybir.dt.bfloat16
ALU = mybir.AluOpType
AF = mybir.ActivationFunctionType
AX = mybir.AxisListType

D = 2048
DFF = 5632
NKC = 16
NFC = 44
SEQH = 4096
TT = 512
NTILE = SEQH // TT
EPS = 1e-6
NS = 8
NSLAB = 4
NCORES = 8


class Res:
    __slots__ = ("name", "w", "r")

    def __init__(self, name):
        self.name = name
        self.w = None
        self.r = []


class Op:
    __slots__ = ("eng", "fn", "deps", "inc", "dma", "semval")

    def __init__(self, eng, fn):
        self.eng = eng
        self.fn = fn
        self.deps = []
        self.inc = False
        self.dma = None
        self.semval = 0


class Prog:
    ENGS = ("pe", "act", "dve", "pool", "sp")

    def __init__(self, nc):
        self.nc = nc
        self.ops = {e: [] for e in self.ENGS}
        self.ndma = {e: 0 for e in self.ENGS}

    def op(self, eng, fn, reads=(), writes=(), dma=False):
        ops = self.ops[eng]
        rec = Op(eng, fn)
        idx = len(ops)
        j = 0
        if dma:
            j = self.ndma[eng]
            self.ndma[eng] = j + 1
            rec.dma = j
            tok = ("d", eng, j)
        else:
            tok = ("e", eng, idx)
        deps = {}
        for r in reads:
            if r.w is not None:
                deps[r.w] = "raw"
        for w in writes:
            if w.w is not None and w.w not in deps:
                deps[w.w] = "waw"
            for t in w.r:
                if t not in deps:
                    deps[t] = "war"
        for t, kind in deps.items():
            if t == tok:
                continue
            if t[0] == "e" and t[1] == eng:
                if eng == "pe":
                    continue
                if kind != "raw" and not dma:
                    continue
            rec.deps.append(t)
            if t[0] == "e":
                self.ops[t[1]][t[2]].inc = True
        if dma and j >= NS:
            rec.deps.append(("d", eng, j - NS))
        for r in reads:
            if tok[0] == "e":
                r.r = [t for t in r.r if not (t[0] == "e" and t[1] == eng)]
            r.r.append(tok)
        for w in writes:
            w.w = tok
            w.r = []
        ops.append(rec)
        return rec

    def barrier(self):
        last = []
        for e in self.ENGS:
            for i in range(len(self.ops[e]) - 1, -1, -1):
                o = self.ops[e][i]
                if o.dma is None and o.fn is not None:
                    last.append(("e", e, i))
                    o.inc = True
                    break
            n = self.ndma[e]
            for j in range(max(0, n - NS), n):
                last.append(("d", e, j))
        for e in self.ENGS:
            rec = Op(e, None)
            rec.deps = list(last)
            self.ops[e].append(rec)

    def emit(self, stack):
        nc = self.nc
        self.esem = {e: stack.enter_context(nc.semaphore("es_" + e)) for e in self.ENGS}
        self.dsem = {
            e: [stack.enter_context(nc.semaphore("ds_%s%d" % (e, i))) for i in range(NS)]
            for e in self.ENGS
            if self.ndma[e] > 0
        }
        for e in self.ENGS:
            c = 0
            for o in self.ops[e]:
                if o.inc:
                    c += 1
                o.semval = c
        block = stack.enter_context(nc.Block())
        engs = {"pe": block.tensor, "act": block.scalar, "dve": block.vector,
                "pool": block.gpsimd, "sp": block.sync}
        for e in self.ENGS:
            def body(eng, e=e):
                self._emit_engine(e, eng)
            engs[e](body)

    def _emit_engine(self, e, eng):
        waited = {}
        for o in self.ops[e]:
            for t in o.deps:
                if t[0] == "e":
                    key = ("e", t[1])
                    sem = self.esem[t[1]]
                    val = self.ops[t[1]][t[2]].semval
                else:
                    key = ("d", t[1], t[2] % NS)
                    sem = self.dsem[t[1]][t[2] % NS]
                    val = 16 * (t[2] // NS + 1)
                if waited.get(key, 0) >= val:
                    continue
                eng.wait_ge(sem, val)
                waited[key] = val
            if o.fn is None:
                continue
            ins = o.fn(eng)
            if o.dma is not None:
                ins.then_inc(self.dsem[e][o.dma % NS], 16)
            elif o.inc:
                ins.then_inc(self.esem[e], 1)
        n = self.ndma[e]
        for j in range(max(0, n - NS), n):
            key = ("d", e, j % NS)
            val = 16 * (j // NS + 1)
            if waited.get(key, 0) >= val:
                continue
            eng.wait_ge(self.dsem[e][j % NS], val)
            waited[key] = val


SLAB_E = 4096
TM = 256
WGROUPS = [
    ("gu00", 88, 2048), ("dn00", 32, 2816), ("in0", 16, 4096), ("out0", 8, 4096),
    ("gu01", 88, 2048), ("dn01", 32, 2816), ("gu10", 88, 2048), ("dn10", 32, 2816),
    ("in1", 24, 4096), ("out1", 8, 4096), ("gu11", 88, 2048), ("dn11", 32, 2816),
]
NEG = -30000.0
BG_GROUPS = ("out0", "gu01", "dn01", "gu10", "dn10", "in1", "out1", "gu11", "dn11")
BG_W = 1024


def full(t):
    return t[tuple(slice(None) for _ in t.shape)]


class K:
    pass


class StopBuild(Exception):
    pass


def chkf(k, n):
    import os
    if float(os.environ.get("KFULL", "99")) <= n:
        raise StopBuild()


def chk(k, n):
    if getattr(k, "level", 99) <= n:
        raise StopBuild()


def group_on(name, layers, ffn_on):
    if name.startswith("gu") or name.startswith("dn"):
        return ffn_on and int(name[2]) in layers
    return int(name[-1]) in layers


def build(ntile=NTILE, layers=(0, 1), ffn_on=True, npre=None):
    nc = bass.Bass("TRN2", target_bir_lowering=False)
    k = K()
    k.nc = nc
    k.ntile = ntile
    k.layers = layers
    k.ffn_on = ffn_on
    seqh = ntile * TT
    k.seqh = seqh
    P = Prog(nc)
    k.P = P

    def din(name, shape, dt=F32):
        return nc.dram_tensor(name, list(shape), dt, kind="ExternalInput").ap()

    def dout(name, shape, dt=F32):
        return nc.dram_tensor(name, list(shape), dt, kind="ExternalOutput").ap()

    def dscr(name, shape, dt):
        return nc.dram_tensor(name, list(shape), dt).ap()

    k.groups = [g for g in WGROUPS if group_on(g[0], layers, ffn_on)]
    I = {}
    I["xpT"] = din("xpT", [128, 16, seqh])
    I["xqT"] = din("xqT", [128, 16, seqh])
    I["xsT"] = din("xsT", [128, 16, 16])
    I["c2T"] = din("c2T", [128, 16, 2])
    I["flag"] = din("flag", [128, 1])
    I["ident"] = din("ident", [128, 128])
    I["cmask"] = din("cmask", [128, 5, 128])
    I["alibi"] = din("alibi", [64, 3 * 512])
    I["norm_gT"] = din("norm_gT", [128, 6, 16])
    I["fnormT"] = din("fnormT", [128, 16])
    I["b_adaT"] = din("b_adaT", [128, 288])
    I["w_ada"] = din("w_ada", [144, 128, 4096])
    I["sinks2"] = din("sinks2", [128, 8])
    I["dtb_b"] = din("dtb_b", [128, 16])
    I["alog_b"] = din("alog_b", [128, 16])
    I["d_b"] = din("d_b", [128, 16])
    I["ng_b"] = din("ng_b", [128, 1024])
    I["conv_wT"] = din("conv_wT", [128, 12, 4])
    I["conv_bT"] = din("conv_bT", [128, 12])
    I["sconv_wT"] = din("sconv_wT", [128, 16, 3])
    I["ckT"] = din("ckT", [128, 128])
    I["cvc"] = din("cvc", [64, 2, 128])
    I["st_hT"] = din("st_hT", [128, 1024])
    I["st_cvT"] = din("st_cvT", [128, 12, 3])
    I["st_scT"] = din("st_scT", [128, 16, 2])
    for name, n, e in k.groups:
        I["w_" + name] = din("w_" + name, [n, 128, e])
    k.I = I
    O = {}
    O["ypT"] = dout("ypT", [128, 16, seqh])
    O["ysT"] = dout("ysT", [128, 16, 16])
    for pfx in ("p", "s"):
        nk = 128 if pfx == "p" else 16
        O["k" + pfx] = dout("o_k" + pfx, [nk, 128])
        O["v" + pfx] = dout("o_v" + pfx, [nk, 128])
        O["h" + pfx] = dout("o_h" + pfx, [128, 1024])
        O["cv" + pfx] = dout("o_cv" + pfx, [128, 12, 3])
        O["sc" + pfx] = dout("o_sc" + pfx, [128, 16, 2])
    import os as _os
    k.dbg = _os.environ.get("KDBG", "") == "1"
    if k.dbg:
        O["d_attnT"] = dout("d_attnT", [128, 8, TM], BF16)
        O["d_yT"] = dout("d_yT", [128, 8, TM], BF16)
        O["d_KT"] = dout("d_KT", [128, 128 + TM], BF16)
        O["d_QrT"] = dout("d_QrT", [128, 8, TM], BF16)
        O["d_kv"] = dout("d_kv", [128, 256], F32)
    k.O = O
    S = {}
    for name, n, e in k.groups:
        S["b_" + name] = dscr("b_" + name, [n * 128, e], BF16)
    k.S = S
    k.wres = {name: [[Res("w_%s_%d" % (name, j))] for j in range(n)] for name, n, e in k.groups}
    k.bg_names = set(BG_GROUPS) if (0 in layers and ffn_on) else set()
    k.bg_jobs, k.bg_pos, k.bg_on, k.bg_slot, k.bg_slots = [], 0, False, 0, 1
    k.wE = {name: e for name, n, e in k.groups}

    with ExitStack() as st:
        k.st = st

        def sb(name, shape, dt):
            return st.enter_context(nc.sbuf_tensor("s_" + name, list(shape), dt))

        def pst(name, shape, dt=F32):
            return st.enter_context(nc.psum_tensor(name, list(shape), dt))

        k.xT = sb("xT", [128, 16, TT], F32)
        k.r_xT = [Res("xT%d" % c) for c in range(16)]
        k.hT = sb("hT", [128, 16, TT], BF16)
        k.r_hT = [Res("hT%d" % c) for c in range(16)]
        k.big = sb("big", [128, NFC * TT], BF16)
        k.r_act = [Res("act%d" % f) for f in range(NFC)]
        k.slab = [sb("slab%d" % j, [128, SLAB_E], BF16) for j in range(NSLAB)]
        k.r_slab = [Res("slab%d" % j) for j in range(NSLAB)]
        k.slab_i = 0
        k.gs = [sb("gs%d" % j, [128, TT], F32) for j in range(4)]
        k.r_gs = [Res("gs%d" % j) for j in range(4)]
        k.sq = k.gs[0:2]
        k.r_sq = k.r_gs[0:2]
        k.tmp = k.gs[0:2]
        k.r_tmp = k.r_gs[0:2]
        k.sgt = k.gs[2:4]
        k.r_sgt = k.r_gs[2:4]
        k.rt = k.gs[2]
        k.r_rt = k.r_gs[2]
        k.rstd = k.gs[3]
        k.r_rstd = k.r_gs[3]
        k.ident = sb("ident", [128, 128], F32)
        k.identb = sb("identb", [128, 128], BF16)
        k.ones = sb("ones", [128, 128], F32)
        k.cmask = sb("cmask", [128, 5, 128], F32)
        k.alibi = sb("alibi", [64, 3 * 512], F32)
        k.modT = sb("modT", [128, 288, 2], F32)
        k.A = sb("A", [128, 6, 16, 2], F32)
        k.SH = sb("SH", [128, 6, 16, 2], F32)
        k.G = sb("G", [128, 6, 16, 2], F32)
        k.norm_gT = sb("norm_gT", [128, 6, 16], F32)
        k.fnormT = sb("fnormT", [128, 16], F32)
        k.b_adaT = sb("b_adaT", [128, 288], F32)
        k.c2T = sb("c2T", [128, 16, 2], F32)
        k.scT = sb("scT", [128, 16, 2], BF16)
        k.flag = sb("flag", [128, 1], F32)
        k.hm = sb("hm", [128, 1], F32)
        k.esink = sb("esink", [128, 8], F32)
        k.dtb = sb("dtb", [128, 16], F32)
        k.aneg = sb("aneg", [128, 16], F32)
        k.d16 = sb("d16", [128, 16], F32)
        k.DT = sb("DT", [128, 1024], F32)
        k.NG = sb("NG", [128, 1024], F32)
        k.cwT = sb("cwT", [128, 12, 4], F32)
        k.cbT = sb("cbT", [128, 12], F32)
        k.swT = sb("swT", [128, 16, 3], F32)
        k.onesk = [sb("onesk%d" % j, [64, 128], BF16) for j in range(2)]
        k.r_const = Res("const")
        k.KT = sb("KT", [128, 128 + TM], BF16)
        k.r_KT = Res("KT")
        k.Va = [sb("Va%d" % j, [64, 2 + TM // 64, 128], BF16) for j in range(2)]
        k.r_Va = Res("Va")
        k.hs = sb("hs", [128, 1024], F32)
        k.r_hs = Res("hs")
        k.hb = [sb("hb%d" % j, [128, 1024], BF16) for j in range(2)]
        k.r_hb = [Res("hb0"), Res("hb1")]
        k.phalo = sb("phalo", [128, 16, 2], F32)
        k.r_phalo = Res("phalo")
        k.cvhalo = sb("cvhalo", [128, 12, 3], F32)
        k.r_cvhalo = Res("cvhalo")
        k.ps = [pst("ps%d" % j, [128, 512]) for j in range(8)]
        k.r_ps = [Res("ps%d" % j) for j in range(8)]

        import os
        k.level = int(os.environ.get("KLEVEL", "99"))
        try:
            phase0(k)
            chk(k, 0)
            P.barrier()
            main_phase(k)
        except StopBuild:
            pass
        P.emit(st)
    return nc


def phase0(k):
    nc, P, I, S = k.nc, k.P, k.I, k.S
    rc = k.r_const
    loads = [(k.ident, "ident"), (k.norm_gT, "norm_gT"), (k.fnormT, "fnormT"), (k.b_adaT, "b_adaT"), (k.c2T, "c2T"),
             (k.flag, "flag"), (k.cmask, "cmask"), (k.alibi, "alibi"), (k.esink, "sinks2"), (k.dtb, "dtb_b"),
             (k.aneg, "alog_b"), (k.d16, "d_b"), (k.NG, "ng_b"), (k.cwT, "conv_wT"), (k.cbT, "conv_bT"), (k.swT, "sconv_wT")]
    for dst, nm in loads:
        P.op("sp", lambda e, dst=dst, nm=nm: e.dma_start(out=full(dst), in_=full(I[nm])), [], [rc], dma=True)
    P.op("pool", lambda e: e.memset(k.ones[:, :], 1.0), [], [rc])
    P.op("dve", lambda e: e.tensor_copy(out=k.identb[:, :], in_=k.ident[:, :]), [rc], [rc])
    P.op("act", lambda e: e.activation(out=k.scT[:, :, :], in_=k.c2T[:, :, :], func=AF.Silu), [rc], [rc])
    P.op("act", lambda e: e.activation(out=k.esink[:, :], in_=k.esink[:, :], func=AF.Exp), [rc], [rc])
    P.op("act", lambda e: e.activation(out=k.aneg[:, :], in_=k.aneg[:, :], func=AF.Exp), [rc], [rc])
    P.op("dve", lambda e: e.tensor_scalar(out=k.aneg[:, :], in0=k.aneg[:, :], scalar1=-1.0, scalar2=None, op0=ALU.mult), [rc], [rc])
    P.op("dve", lambda e: e.tensor_scalar(out=k.hm[:, :], in0=k.flag[:, :], scalar1=-1.0, scalar2=-NEG, op0=ALU.add, op1=ALU.mult),
         [rc], [rc])
    P.op("dve", lambda e: e.tensor_copy(out=k.DT[:, :].rearrange("p (j q) -> p j q", q=64),
                                        in_=k.d16[:, :].unsqueeze(2).to_broadcast([128, 16, 64])), [rc], [rc])
    for j in range(2):
        P.op("pool", lambda e, j=j: e.memset(k.onesk[j][:, :], 0.0), [], [rc])
        P.op("pool", lambda e, j=j: e.memset(k.onesk[j][:, j * 64:(j + 1) * 64], 1.0), [], [rc])
        P.op("pool", lambda e, j=j: e.memset(full(k.Va[j]), 0.0), [], [k.r_Va])
    P.op("pool", lambda e: e.memset(k.KT[:, :], 0.0), [], [k.r_KT])
    P.op("pool", lambda e: e.memset(k.hs[:, :], 0.0), [], [k.r_hs])
    P.op("pool", lambda e: e.memset(k.hb[0][:, :], 0.0), [], [k.r_hb[0]])
    P.op("pool", lambda e: e.memset(full(k.phalo), 0.0), [], [k.r_phalo])
    P.op("pool", lambda e: e.memset(full(k.cvhalo), 0.0), [], [k.r_cvhalo])

    stg_f = [k.xT[:, 0:8, :].rearrange("p a b -> p (a b)"), k.xT[:, 8:16, :].rearrange("p a b -> p (a b)")]
    r_stg_f = [Res("stgf0"), Res("stgf1")]
    stg_b = k.slab
    r_stg_b = k.r_slab
    cast_engs = ("dve", "pool", "act")
    n = 0

    def cast(eng, dst, src):
        if eng == "act":
            return lambda e: e.activation(out=dst, in_=src, func=AF.Copy)
        return lambda e: e.tensor_copy(out=dst, in_=src)

    pm = [k.ps[6], k.ps[7]]
    r_pm = [k.r_ps[6], k.r_ps[7]]
    for sl in range(144):
        if (sl // 72) not in k.layers:
            continue
        fb = n % 2
        bb = n % NSLAB
        P.op("sp", lambda e, fb=fb, sl=sl: e.dma_start(out=stg_f[fb], in_=I["w_ada"][sl]), [], [r_stg_f[fb]], dma=True)
        ce = cast_engs[n % 3]
        P.op(ce, cast(ce, stg_b[bb][:, :], stg_f[fb]), [r_stg_f[fb]], [r_stg_b[bb]])
        for ch in range(2):
            j = sl * 2 + ch
            pj = j % 2
            for kc in range(16):
                P.op("pe", lambda e, bb=bb, kc=kc, ch=ch, pj=pj: e.matmul(
                    pm[pj][:, 0:2], stg_b[bb][:, kc * 256 + ch * 128: kc * 256 + ch * 128 + 128],
                    k.scT[:, kc, :], start=(kc == 0), stop=(kc == 15)), [r_stg_b[bb], rc], [r_pm[pj]])
            P.op("act", lambda e, j=j, pj=pj: e.activation(out=k.modT[:, j, :], in_=pm[pj][:, 0:2], func=AF.Identity,
                                                          bias=k.b_adaT[:, j:j + 1], scale=1.0), [r_pm[pj], rc], [rc])
        n += 1
    for l in k.layers:
        for i in range(3):
            li = l * 3 + i
            j0 = l * 144 + (3 * i) * 16
            P.op("dve", lambda e, li=li, j0=j0: e.tensor_copy(out=k.SH[:, li, :, :], in_=k.modT[:, j0:j0 + 16, :]), [rc], [rc])
            P.op("dve", lambda e, li=li, j0=j0: e.tensor_scalar(out=k.A[:, li, :, :], in0=k.modT[:, j0 + 16:j0 + 32, :],
                                                                 scalar1=1.0, scalar2=None, op0=ALU.add), [rc], [rc])
            for sq_ in range(2):
                P.op("dve", lambda e, li=li, sq_=sq_: e.tensor_tensor(out=k.A[:, li, :, sq_], in0=k.A[:, li, :, sq_],
                                                                      in1=k.norm_gT[:, li, :], op=ALU.mult), [rc], [rc])
            gs = 1.0 if i == 1 else 0.5
            P.op("dve", lambda e, li=li, j0=j0, gs=gs: e.tensor_scalar(out=k.G[:, li, :, :], in0=k.modT[:, j0 + 32:j0 + 48, :],
                                                                        scalar1=gs, scalar2=None, op0=ALU.mult), [rc], [rc])
    for name, nsl, el in k.groups:
        if name in k.bg_names:
            continue
        per = max(1, SLAB_E // el)
        s = 0
        while s < nsl:
            cnt = min(per, nsl - s)
            fb = n % 2
            bb = n % NSLAB
            src = I["w_" + name][s:s + cnt].rearrange("s p e -> p s e")
            dstd = S["b_" + name].rearrange("(s p) e -> p s e", p=128)[:, s:s + cnt, :]
            sf = stg_f[fb][:, 0:cnt * el]
            sbf = stg_b[bb][:, 0:cnt * el]
            P.op("sp", lambda e, sf=sf, src=src, cnt=cnt: e.dma_start(out=sf.rearrange("p (s e) -> p s e", s=cnt), in_=src),
                 [], [r_stg_f[fb]], dma=True)
            ce = cast_engs[n % 3]
            P.op(ce, cast(ce, sbf, sf), [r_stg_f[fb]], [r_stg_b[bb]])
            P.op("pool", lambda e, sbf=sbf, dstd=dstd, cnt=cnt: e.dma_start(out=dstd, in_=sbf.rearrange("p (s e) -> p s e", s=cnt)),
                 [r_stg_b[bb]], [r for j in range(s, s + cnt) for r in k.wres[name][j]], dma=True)
            n += 1
            s += cnt


def load_slab(k, name, idx, cnt=1):
    P = k.P
    j = k.slab_i % NSLAB
    k.slab_i += 1
    el = k.wE[name]
    src = k.S["b_" + name].rearrange("(s p) e -> p s e", p=128)[:, idx:idx + cnt, :]
    buf = k.slab[j]
    P.op("sp", lambda e: e.dma_start(out=buf[:, 0:cnt * el].rearrange("p (s e) -> p s e", s=cnt), in_=src),
         [r for i in range(idx, idx + cnt) for r in k.wres[name][i]], [k.r_slab[j]], dma=True)
    return buf, k.r_slab[j]


def make_bg_jobs(k):
    jobs = []
    for name, nsl, el in k.groups:
        if name not in k.bg_names:
            continue
        for sidx in range(nsl):
            a = 0
            pieces = []
            while a < el:
                b = min(el, a + BG_W)
                r = Res("w_%s_%d_%d" % (name, sidx, a))
                pieces.append(r)
                jobs.append((name, sidx, a, b, r))
                a = b
            k.wres[name][sidx] = pieces
    k.bg_jobs = jobs
    k.bg_pos = 0
    k.bg_in_pos = 0


def _bg_bufs(k, j):
    W = k.wk
    i = j % 2
    sf, rsf = ((W.y, W.r_y), (W.xs, W.r_xs))[i]
    sbf, rsb = ((W.MT[:, 0:8, :].rearrange("p a b -> p (a b)"), W.r_MT), (W.yn, W.r_yn))[i]
    return sf, rsf, sbf, rsb


def _bg_in(k):
    j = k.bg_in_pos
    if j >= len(k.bg_jobs):
        return
    k.bg_in_pos += 1
    name, sidx, a, b, r = k.bg_jobs[j]
    w = b - a
    sf, rsf, sbf, rsb = _bg_bufs(k, j)
    src = k.I["w_" + name][sidx][:, a:b]
    k.P.op("pool", lambda e, sf=sf, src=src, w=w: e.dma_start(out=sf[:, 0:w], in_=src), [], [rsf], dma=True)


def bg_step(k, n):
    P = k.P
    for _ in range(n):
        if k.bg_pos >= len(k.bg_jobs):
            return
        j = k.bg_pos
        k.bg_pos += 1
        if k.bg_in_pos <= j:
            _bg_in(k)
        _bg_in(k)
        name, sidx, a, b, r = k.bg_jobs[j]
        w = b - a
        sf, rsf, sbf, rsb = _bg_bufs(k, j)
        dst = k.S["b_" + name][sidx * 128:(sidx + 1) * 128, a:b]
        P.op("pool", lambda e, sf=sf, sbf=sbf, w=w: e.tensor_copy(out=sbf[:, 0:w], in_=sf[:, 0:w]), [rsf], [rsb])
        P.op("pool", lambda e, sbf=sbf, dst=dst, w=w: e.dma_start(out=dst, in_=sbf[:, 0:w]), [rsb], [r], dma=True)


def bg_tick(k):
    if not k.bg_on:
        return
    k.bg_slot += 1
    target = -(-k.bg_slot * len(k.bg_jobs) // k.bg_slots)
    bg_step(k, max(0, min(target, len(k.bg_jobs)) - k.bg_pos))


def norm_mod(k, li, seq, Tn, final=False):
    P = k.P
    rc = k.r_const
    pss, r_pss = k.ps[6], k.r_ps[6]
    for c in range(16):
        q = c % 2
        P.op("act", lambda e, c=c, q=q: e.activation(out=k.sq[q][:, 0:Tn], in_=k.xT[:, c, 0:Tn], func=AF.Square),
             [k.r_xT[c]], [k.r_sq[q]])
        P.op("pe", lambda e, c=c, q=q: e.matmul(pss[:, 0:Tn], k.ones[:, :], k.sq[q][:, 0:Tn], start=(c == 0), stop=(c == 15)),
             [k.r_sq[q], rc], [r_pss])
    P.op("act", lambda e: e.activation(out=k.rt[:, 0:Tn], in_=pss[:, 0:Tn], func=AF.Sqrt, bias=EPS, scale=1.0 / D),
         [r_pss], [k.r_rt])
    P.op("dve", lambda e: e.reciprocal(out=k.rstd[:, 0:Tn], in_=k.rt[:, 0:Tn]), [k.r_rt], [k.r_rstd])
    for c in range(16):
        q = c % 2
        if final:
            P.op("dve", lambda e, c=c: e.scalar_tensor_tensor(out=k.xT[:, c, 0:Tn], in0=k.xT[:, c, 0:Tn], scalar=k.fnormT[:, c:c + 1],
                                                              in1=k.rstd[:, 0:Tn], op0=ALU.mult, op1=ALU.mult),
                 [k.r_xT[c], k.r_rstd, rc], [k.r_xT[c]])
        else:
            P.op("dve", lambda e, c=c, q=q: e.scalar_tensor_tensor(out=k.tmp[q][:, 0:Tn], in0=k.xT[:, c, 0:Tn],
                                                                   scalar=k.A[:, li, c, seq:seq + 1], in1=k.rstd[:, 0:Tn],
                                                                   op0=ALU.mult, op1=ALU.mult),
                 [k.r_xT[c], k.r_rstd, rc], [k.r_tmp[q]])
            P.op("act", lambda e, c=c, q=q: e.activation(out=k.hT[:, c, 0:Tn], in_=k.tmp[q][:, 0:Tn], func=AF.Identity,
                                                         bias=k.SH[:, li, c, seq:seq + 1], scale=1.0),
                 [k.r_tmp[q], rc], [k.r_hT[c]])


def ffn(k, l, i, seq, Tn):
    P = k.P
    rc = k.r_const
    if not k.ffn_on:
        return
    li = l * 3 + (0 if i == 0 else 2)
    actT = k.big
    gname = "gu%d%d" % (l, i)
    dname = "dn%d%d" % (l, i)
    for s in range(22):
        bg, rg = load_slab(k, gname, 2 * s, 2)
        bu, ru = load_slab(k, gname, 44 + 2 * s, 2)
        for ch in range(2):
            f = 2 * s + ch
            q = f % 2
            pg, rpg = k.ps[q], k.r_ps[q]
            pu, rpu = k.ps[2 + q], k.r_ps[2 + q]
            for (w, rw, pp, rpp) in ((bg, rg, pg, rpg), (bu, ru, pu, rpu)):
                for kc in range(16):
                    P.op("pe", lambda e, w=w, pp=pp, kc=kc, ch=ch: e.matmul(
                        pp[:, 0:Tn], w[:, ch * 2048 + kc * 128: ch * 2048 + kc * 128 + 128], k.hT[:, kc, 0:Tn],
                        start=(kc == 0), stop=(kc == 15)), [rw, k.r_hT[kc]], [rpp])
            P.op("act", lambda e, q=q, pg=pg: e.activation(out=k.sgt[q][:, 0:Tn], in_=pg[:, 0:Tn], func=AF.Silu),
                 [rpg], [k.r_sgt[q]])
            P.op("dve", lambda e, q=q, pu=pu, f=f: e.tensor_tensor(out=actT[:, f * TT: f * TT + Tn], in0=k.sgt[q][:, 0:Tn],
                                                                   in1=pu[:, 0:Tn], op=ALU.mult),
                 [k.r_sgt[q], rpu, k.r_cvhalo], [k.r_act[f]])
        bg_tick(k)
    for dc in range(16):
        q = dc % 2
        pd, rpd = k.ps[4 + q], k.r_ps[4 + q]
        for half in range(2):
            bd, rd = load_slab(k, dname, dc * 2 + half, 1)
            for kc in range(22):
                f = half * 22 + kc
                P.op("pe", lambda e, bd=bd, pd=pd, kc=kc, f=f, half=half: e.matmul(
                    pd[:, 0:Tn], bd[:, kc * 128:(kc + 1) * 128], actT[:, f * TT: f * TT + Tn],
                    start=(half == 0 and kc == 0), stop=(half == 1 and kc == 21)), [rd, k.r_act[f]], [rpd])
        P.op("dve", lambda e, dc=dc, pd=pd: e.scalar_tensor_tensor(out=k.xT[:, dc, 0:Tn], in0=pd[:, 0:Tn],
                                                                   scalar=k.G[:, li, dc, seq:seq + 1], in1=k.xT[:, dc, 0:Tn],
                                                                   op0=ALU.mult, op1=ALU.add),
             [rpd, k.r_xT[dc], rc], [k.r_xT[dc]])
        bg_tick(k)


def carve(k):
    if hasattr(k, "cv"):
        return k.cv
    cv = K()
    off = [0]

    def take(n_bf16):
        a = off[0]
        off[0] += n_bf16
        assert off[0] <= NFC * TT, off[0]
        return k.big[:, a:a + n_bf16]

    cv.QrT = take(8 * TM).rearrange("p (g t) -> p g t", g=8)
    cv.attnT = take(8 * TM).rearrange("p (g t) -> p g t", g=8)
    cv.yT = take(8 * TM).rearrange("p (g t) -> p g t", g=8)
    cv.xbcT = take(2 * 12 * (TM + 4)).bitcast(F32).rearrange("p (c t) -> p c t", c=12)
    cv.dtraw = take(2 * 2 * 16).bitcast(F32).rearrange("p (b j) -> p b j", b=2)
    cv.sz = take(2 * 2 * 1024).bitcast(F32).rearrange("p (b j) -> p b j", b=2)
    cv.R = take(2 * 8 * 128).bitcast(F32).rearrange("p (j q) -> p j q", j=8)
    cv.Lh = take(2 * 8 * 128).bitcast(F32).rearrange("p (j q) -> p j q", j=8)
    cv.r_QrT, cv.r_attnT, cv.r_yT, cv.r_xbcT, cv.r_dtraw = Res("QrT"), Res("attnT"), Res("yT"), Res("xbcT"), Res("dtraw")
    cv.r_sz, cv.r_R, cv.r_Lh = Res("sz"), Res("R"), Res("Lh")
    k.cv = cv
    return cv


def mixer0(k, seq, Tn, c0, kind, first_main):
    P = k.P
    rc = k.r_const
    cv = carve(k)
    st = k.st_m0
    li = 1
    pre = kind == "pre"
    nblk = (Tn + 127) // 128
    Lc = 64 if Tn >= 64 else Tn
    nch = Tn // Lc
    pq = [0]

    def bank():
        j = pq[0] % 4
        pq[0] += 1
        return k.ps[j], k.r_ps[j]

    def fm_group(buf, rbuf, coff, evac):
        pp, rpp = bank()
        for kc in range(16):
            P.op("pe", lambda e, kc=kc, pp=pp: e.matmul(pp[:, 0:Tn], buf[:, kc * 256 + coff: kc * 256 + coff + 128],
                                                        k.hT[:, kc, c0:c0 + Tn], start=(kc == 0), stop=(kc == 15)),
                 [rbuf, k.r_hT[kc]], [rpp])
        evac(pp, rpp)

    P.op("pool", lambda e: e.tensor_copy(out=cv.xbcT[:, :, 0:3], in_=full(k.cvhalo)), [k.r_cvhalo, k.r_hT[15]], [cv.r_xbcT])
    if not pre:
        for s in range(4):
            buf, rbuf = load_slab(k, "in0", s)
            for ch in range(2):
                g = 2 * s + ch
                fm_group(buf, rbuf, ch * 128, lambda pp, rpp, g=g: P.op(
                    "act", lambda e: e.activation(out=cv.QrT[:, g, 0:Tn], in_=pp[:, 0:Tn], func=AF.Copy, scale=0.125),
                    [rpp], [cv.r_QrT]))
    buf, rbuf = load_slab(k, "in0", 4)
    fm_group(buf, rbuf, 0, lambda pp, rpp: P.op(
        "act", lambda e: e.activation(out=k.KT[:, 128:128 + Tn], in_=pp[:, 0:Tn], func=AF.Copy), [rpp], [k.r_KT]))
    for c in range(nch):
        pp, rpp = bank()
        for kc in range(16):
            P.op("pe", lambda e, kc=kc, pp=pp, c=c, buf=buf: e.matmul(pp[0:Lc, 0:128], k.hT[:, kc, c0 + c * Lc:c0 + (c + 1) * Lc],
                                                            buf[:, kc * 256 + 128: kc * 256 + 256], start=(kc == 0), stop=(kc == 15)),
                 [rbuf, k.r_hT[kc]], [rpp])
        for j in range(2):
            P.op("act", lambda e, pp=pp, c=c, j=j: e.activation(out=k.Va[j][0:Lc, 2 + c, j * 64:(j + 1) * 64],
                                                                in_=pp[0:Lc, j * 64:(j + 1) * 64], func=AF.Copy), [rpp], [k.r_Va])
    if st.get("kv_out") is not None and c0 + Tn >= st["Tn"]:
        nk = min(128, Tn)
        t0 = Tn - nk
        pp, rpp = bank()
        for kc in range(16):
            P.op("pe", lambda e, kc=kc, pp=pp, buf=buf: e.matmul(pp[0:nk, 0:256], k.hT[:, kc, c0 + t0:c0 + Tn], buf[:, kc * 256: kc * 256 + 256],
                                                        start=(kc == 0), stop=(kc == 15)), [rbuf, k.r_hT[kc]], [rpp])
        P.op("act", lambda e, pp=pp: e.activation(out=k.sgt[0][0:nk, 0:256], in_=pp[0:nk, 0:256], func=AF.Copy), [rpp], [k.r_sgt[0]])
        ko, vo = st["kv_out"]
        P.op("pool", lambda e: e.dma_start(out=ko[:, :], in_=k.sgt[0][0:nk, 0:128]), [k.r_sgt[0]], [], dma=True)
        P.op("pool", lambda e: e.dma_start(out=vo[:, :], in_=k.sgt[0][0:nk, 128:256]), [k.r_sgt[0]], [], dma=True)
    if not pre:
        for s in range(4):
            buf, rbuf = load_slab(k, "in0", 5 + s)
            for b in range(nblk):
                bs = min(128, Tn - b * 128)
                pp, rpp = bank()
                for kc in range(16):
                    P.op("pe", lambda e, kc=kc, pp=pp, b=b, bs=bs, buf=buf: e.matmul(
                        pp[0:bs, 0:256], k.hT[:, kc, c0 + b * 128:c0 + b * 128 + bs], buf[:, kc * 256: kc * 256 + 256],
                        start=(kc == 0), stop=(kc == 15)), [rbuf, k.r_hT[kc]], [rpp])
                P.op("act", lambda e, pp=pp, b=b, bs=bs, s=s: e.activation(out=cv.sz[0:bs, b, s * 256:(s + 1) * 256], in_=pp[0:bs, 0:256],
                                                                           func=AF.Silu), [rpp], [cv.r_sz])
    for s in range(6):
        if pre and s == 5:
            continue
        buf, rbuf = load_slab(k, "in0", 9 + s)
        for ch in range(2):
            c12 = 2 * s + ch
            fm_group(buf, rbuf, ch * 128, lambda pp, rpp, c12=c12: P.op(
                "act", lambda e: e.activation(out=cv.xbcT[:, c12, 3:3 + Tn], in_=pp[:, 0:Tn], func=AF.Copy), [rpp], [cv.r_xbcT]))
    buf, rbuf = load_slab(k, "in0", 15)
    for b in range(nblk):
        bs = min(128, Tn - b * 128)
        pp, rpp = bank()
        for kc in range(16):
            P.op("pe", lambda e, kc=kc, pp=pp, b=b, bs=bs, buf=buf: e.matmul(pp[0:bs, 0:16], k.hT[:, kc, c0 + b * 128:c0 + b * 128 + bs],
                                                                   buf[:, kc * 256: kc * 256 + 16], start=(kc == 0), stop=(kc == 15)),
                 [rbuf, k.r_hT[kc]], [rpp])
        P.op("dve", lambda e, pp=pp, b=b, bs=bs: e.tensor_tensor(out=cv.dtraw[0:bs, b, :], in0=pp[0:bs, 0:16], in1=k.dtb[0:bs, :],
                                                                 op=ALU.add), [rpp, rc], [cv.r_dtraw])
    chk(k, 1)
    if not pre:
        chkf(k, 1)
        attention(k, Tn, Lc, nch, first_main)
        chkf(k, 4)
    for b in range(nblk):
        bs = min(128, Tn - b * 128)
        ssd_block(k, b, bs, Lc, pre)
        chk(k, 2 if pre else 6)
    if k.dbg and st.get("kv_out") is not None and c0 + Tn >= st["Tn"] and Tn == TM:
        O_ = k.O
        P.op("pool", lambda e: e.dma_start(out=full(O_["d_attnT"]), in_=full(cv.attnT)), [cv.r_attnT], [], dma=True)
        P.op("pool", lambda e: e.dma_start(out=full(O_["d_yT"]), in_=full(cv.yT)), [cv.r_yT], [], dma=True)
        P.op("pool", lambda e: e.dma_start(out=full(O_["d_KT"]), in_=full(k.KT)), [k.r_KT], [], dma=True)
        P.op("pool", lambda e: e.dma_start(out=full(O_["d_QrT"]), in_=full(cv.QrT)), [cv.r_QrT], [], dma=True)
    if Tn >= 128:
        P.op("pool", lambda e: e.tensor_copy(out=k.KT[:, 0:128], in_=k.KT[:, Tn:Tn + 128]), [k.r_KT], [k.r_KT])
        for j in range(2):
            P.op("pool", lambda e, j=j: e.tensor_copy(out=k.Va[j][:, 0:2, :], in_=k.Va[j][:, nch:nch + 2, :]), [k.r_Va], [k.r_Va])
    P.op("pool", lambda e: e.tensor_copy(out=full(k.cvhalo), in_=cv.xbcT[:, :, Tn:Tn + 3]), [cv.r_xbcT], [k.r_cvhalo])
    chk(k, 3)
    if pre:
        return
    for s in range(8):
        buf, rbuf = load_slab(k, "out0", s)
        for ch in range(2):
            dc = 2 * s + ch
            pp, rpp = k.ps[4 + dc % 2], k.r_ps[4 + dc % 2]
            for kc in range(16):
                rhs = cv.attnT[:, kc, 0:Tn] if kc < 8 else cv.yT[:, kc - 8, 0:Tn]
                rr = cv.r_attnT if kc < 8 else cv.r_yT
                P.op("pe", lambda e, kc=kc, pp=pp, rhs=rhs, ch=ch, buf=buf: e.matmul(
                    pp[:, 0:Tn], buf[:, kc * 256 + ch * 128: kc * 256 + ch * 128 + 128], rhs, start=(kc == 0), stop=(kc == 15)),
                    [rbuf, rr], [rpp])
            P.op("dve", lambda e, dc=dc, pp=pp: e.scalar_tensor_tensor(out=k.xT[:, dc, c0:c0 + Tn], in0=pp[:, 0:Tn],
                                                                       scalar=k.G[:, li, dc, seq:seq + 1], in1=k.xT[:, dc, c0:c0 + Tn],
                                                                       op0=ALU.mult, op1=ALU.add),
                 [rpp, k.r_xT[dc], rc], [k.r_xT[dc]])


def attention(k, Tn, Lc, nch, first_main):
    P = k.P
    rc = k.r_const
    cv = carve(k)
    NQ = 8 * Lc
    for c in range(nch):
        t0 = c * Lc
        pts = []
        n = 0
        for kv in range(2):
            for m in (2, 1, 0):
                slot = c + 2 - m
                nk = Lc if m == 0 else 64
                pp, rpp = k.ps[n % 2], k.r_ps[n % 2]
                P.op("pe", lambda e, pp=pp, kv=kv, slot=slot, nk=nk, t0=t0: e.matmul(
                    pp[0:nk, 0:NQ], k.KT[kv * 64:(kv + 1) * 64, slot * 64: slot * 64 + nk],
                    cv.QrT[kv * 64:(kv + 1) * 64, :, t0:t0 + Lc], start=True, stop=True), [k.r_KT, cv.r_QrT], [rpp])
                tS, rtS = k.tmp[n % 2], k.r_tmp[n % 2]
                al = k.alibi[0:nk, m * 512:(m + 1) * 512].rearrange("p (g q) -> p g q", g=8)[:, :, 0:Lc]
                tSv = tS[0:nk, 0:NQ].rearrange("p (g q) -> p g q", g=8)
                ppv = pp[0:nk, 0:NQ].rearrange("p (g q) -> p g q", g=8)
                if kv == 0:
                    P.op("dve", lambda e, tSv=tSv, ppv=ppv, al=al: e.tensor_tensor(out=tSv, in0=ppv, in1=al, op=ALU.add),
                         [rpp, rc], [rtS])
                else:
                    P.op("dve", lambda e, tSv=tSv, ppv=ppv, al=al: e.scalar_tensor_tensor(out=tSv, in0=al, scalar=0.0625, in1=ppv,
                                                                                          op0=ALU.mult, op1=ALU.add),
                         [rpp, rc], [rtS])
                pt, rpt = k.PT[n], k.r_PT[n]
                masked = first_main and slot < 2
                if masked:
                    P.op("act", lambda e, pt=pt, tS=tS, nk=nk: e.activation(out=pt[0:nk, 0:NQ], in_=tS[0:nk, 0:NQ], func=AF.Exp,
                                                                           bias=k.hm[0:nk, 0:1], scale=1.0), [rtS, rc], [rpt])
                else:
                    P.op("act", lambda e, pt=pt, tS=tS, nk=nk: e.activation(out=pt[0:nk, 0:NQ], in_=tS[0:nk, 0:NQ], func=AF.Exp),
                         [rtS], [rpt])
                pts.append((pt, rpt, kv, slot, nk))
                n += 1
        chkf(k, 2)
        po, rpo = k.ps[2], k.r_ps[2]
        pd, rpd = k.ps[3], k.r_ps[3]
        for i, (pt, rpt, kv, slot, nk) in enumerate(pts):
            P.op("pe", lambda e, pt=pt, kv=kv, slot=slot, nk=nk, i=i: e.matmul(po[:, 0:NQ], k.Va[kv][0:nk, slot, :], pt[0:nk, 0:NQ],
                                                                              start=(i == 0), stop=(i == 5)), [rpt, k.r_Va], [rpo])
        for i, (pt, rpt, kv, slot, nk) in enumerate(pts):
            P.op("pe", lambda e, pt=pt, kv=kv, nk=nk, i=i: e.matmul(pd[:, 0:NQ], k.onesk[kv][0:nk, :], pt[0:nk, 0:NQ],
                                                                   start=(i == 0), stop=(i == 5)), [rpt, rc], [rpd])
        chkf(k, 3)
        den, rden = k.sgt[0], k.r_sgt[0]
        denv = den[:, 0:NQ].rearrange("p (g q) -> p g q", g=8)
        P.op("dve", lambda e, denv=denv: e.tensor_tensor(out=denv, in0=pd[:, 0:NQ].rearrange("p (g q) -> p g q", g=8),
                                                         in1=k.esink[:, :].unsqueeze(2).to_broadcast([128, 8, Lc]), op=ALU.add),
             [rpd, rc], [rden])
        chkf(k, 3.3)
        P.op("dve", lambda e, den=den: e.reciprocal(out=den[:, 0:NQ], in_=den[:, 0:NQ]), [rden], [rden])
        chkf(k, 3.6)
        P.op("dve", lambda e, denv=denv, t0=t0: e.tensor_tensor(out=cv.attnT[:, :, t0:t0 + Lc],
                                                                in0=po[:, 0:NQ].rearrange("p (g q) -> p g q", g=8), in1=denv, op=ALU.mult),
             [rpo, rden], [cv.r_attnT])
        chkf(k, 3.9)


def ssd_block(k, b, bs, Lc, pre):
    P = k.P
    rc = k.r_const
    cv = carve(k)
    tb = b * 128
    W = k.wk
    LE, SU, BLK, SEL0, SEL1 = (k.cmask[:, i, :] for i in range(5))
    nchb = max(1, bs // 64)
    u = cv.dtraw[0:bs, b, :]
    P.op("act", lambda e: e.activation(out=W.a16[0:bs, :], in_=u, func=AF.Abs), [cv.r_dtraw], [W.r_a16])
    P.op("act", lambda e: e.activation(out=W.a16[0:bs, :], in_=W.a16[0:bs, :], func=AF.Exp, scale=-1.0), [W.r_a16], [W.r_a16])
    P.op("act", lambda e: e.activation(out=W.a16[0:bs, :], in_=W.a16[0:bs, :], func=AF.Ln, bias=1.0, scale=1.0), [W.r_a16], [W.r_a16])
    P.op("dve", lambda e: e.scalar_tensor_tensor(out=W.dt[0:bs, :], in0=u, scalar=0.0, in1=W.a16[0:bs, :], op0=ALU.max, op1=ALU.add),
         [cv.r_dtraw, W.r_a16], [W.r_dt])
    P.op("dve", lambda e: e.tensor_tensor(out=W.dA[0:bs, :], in0=W.dt[0:bs, :], in1=k.aneg[0:bs, :], op=ALU.mult), [W.r_dt, rc], [W.r_dA])
    pc, rpc = k.ps[0], k.r_ps[0]
    P.op("pe", lambda e: e.matmul(pc[0:bs, 0:16], LE[0:bs, 0:bs], W.dA[0:bs, :], start=True, stop=True), [W.r_dA, rc], [rpc])
    P.op("pe", lambda e: e.matmul(pc[0:bs, 16:32], BLK[0:bs, 0:bs], W.dA[0:bs, :], start=True, stop=True), [W.r_dA, rc], [rpc])
    P.op("pe", lambda e: e.matmul(pc[:, 32:48], SEL0[0:bs, :], W.dA[0:bs, :], start=True, stop=True), [W.r_dA, rc], [rpc])
    if nchb == 2:
        P.op("pe", lambda e: e.matmul(pc[:, 48:64], SEL1[0:bs, :], W.dA[0:bs, :], start=True, stop=True), [W.r_dA, rc], [rpc])
    P.op("act", lambda e: e.activation(out=W.s64[0:bs, 0:16], in_=pc[0:bs, 0:16], func=AF.Copy), [rpc], [W.r_s64])
    P.op("act", lambda e: e.activation(out=W.s64[0:bs, 16:32], in_=pc[0:bs, 0:16], func=AF.Exp), [rpc], [W.r_s64])
    P.op("dve", lambda e: e.tensor_tensor(out=W.s64[0:bs, 32:48], in0=pc[0:bs, 16:32], in1=W.s64[0:bs, 0:16], op=ALU.subtract),
         [rpc, W.r_s64], [W.r_s64])
    P.op("act", lambda e: e.activation(out=W.s64[0:bs, 32:48], in_=W.s64[0:bs, 32:48], func=AF.Exp), [W.r_s64], [W.r_s64])
    P.op("dve", lambda e: e.tensor_tensor(out=W.s64[0:bs, 32:48], in0=W.s64[0:bs, 32:48], in1=W.dt[0:bs, :], op=ALU.mult),
         [W.r_s64, W.r_dt], [W.r_s64])
    P.op("act", lambda e: e.activation(out=W.cdb[:, 0:16 * nchb], in_=pc[:, 32:32 + 16 * nchb], func=AF.Exp), [rpc], [W.r_cdb])
    dt_b = W.dt[0:bs, :]
    dtd_b = W.s64[0:bs, 32:48]
    expcum = W.s64[0:bs, 16:32]
    if not pre:
        chkf(k, 4.1)
    for c12 in range(12):
        if pre and c12 >= 10:
            continue
        t, rt_ = k.tmp[c12 % 2], k.r_tmp[c12 % 2]
        xin = cv.xbcT
        P.op("dve", lambda e, t=t, c12=c12: e.tensor_scalar(out=t[:, 0:bs], in0=xin[:, c12, tb:tb + bs], scalar1=k.cwT[:, c12, 0:1],
                                                            scalar2=None, op0=ALU.mult), [cv.r_xbcT, rc], [rt_])
        for i in (1, 2, 3):
            P.op("dve", lambda e, t=t, c12=c12, i=i: e.scalar_tensor_tensor(out=t[:, 0:bs], in0=xin[:, c12, tb + i:tb + i + bs],
                                                                            scalar=k.cwT[:, c12, i:i + 1], in1=t[:, 0:bs],
                                                                            op0=ALU.mult, op1=ALU.add), [cv.r_xbcT, rt_, rc], [rt_])
        if c12 < 8:
            P.op("act", lambda e, t=t, c12=c12: e.activation(out=W.xc[:, c12, 0:bs], in_=t[:, 0:bs], func=AF.Silu,
                                                             bias=k.cbT[:, c12:c12 + 1], scale=1.0), [rt_, rc], [W.r_xc])
        elif c12 < 10:
            P.op("act", lambda e, t=t, c12=c12: e.activation(out=W.BT[:, c12 - 8, 0:bs], in_=t[:, 0:bs], func=AF.Silu,
                                                             bias=k.cbT[:, c12:c12 + 1], scale=1.0), [rt_, rc], [W.r_BT])
        else:
            P.op("act", lambda e, t=t, c12=c12: e.activation(out=W.CT[:, c12 - 10, 0:bs], in_=t[:, 0:bs], func=AF.Silu,
                                                             bias=k.cbT[:, c12:c12 + 1], scale=1.0), [rt_, rc], [W.r_CT])
    if not pre:
        chkf(k, 4.2)
    for half in range(2):
        pt, rpt = k.ps[2 + half], k.r_ps[2 + half]
        for i in range(4):
            c = half * 4 + i
            P.op("pe", lambda e, pt=pt, c=c, i=i: e.transpose(pt[0:bs, i * 128:(i + 1) * 128], W.xc[:, c, 0:bs], k.ident[:, :]),
                 [W.r_xc, rc], [rpt])
        ptv = pt[0:bs, :].rearrange("p (j q) -> p j q", q=64)
        hs_ = slice(half * 8, half * 8 + 8)
        cs_ = slice(half * 512, half * 512 + 512)
        import os
        SK = os.environ.get("KSKIP", "")
        if not pre:
            if "xdt" not in SK:
                P.op("dve", lambda e, ptv=ptv, hs_=hs_, cs_=cs_: e.tensor_tensor(
                    out=W.xdt[0:bs, cs_].rearrange("p (j q) -> p j q", q=64), in0=ptv,
                    in1=dt_b[:, hs_].unsqueeze(2).to_broadcast([bs, 8, 64]), op=ALU.mult), [rpt, W.r_dt], [W.r_xdt])
            if "xs" not in SK:
                P.op("dve", lambda e, pt=pt, cs_=cs_: e.tensor_tensor(out=W.xs[0:bs, cs_], in0=pt[0:bs, :], in1=k.DT[0:bs, cs_],
                                                                      op=ALU.mult), [rpt, rc], [W.r_xs])
        P.op("dve", lambda e, ptv=ptv, hs_=hs_, cs_=cs_: e.tensor_tensor(
            out=W.xdtd[0:bs, cs_].rearrange("p (j q) -> p j q", q=64), in0=ptv,
            in1=dtd_b[:, hs_].unsqueeze(2).to_broadcast([bs, 8, 64]), op=ALU.mult), [rpt, W.r_s64], [W.r_xdtd])
    if not pre:
        chkf(k, 4.3)
    pbt, rpbt = k.ps[1], k.r_ps[1]
    pbtb = pbt[:, :].bitcast(BF16)
    for g in range(2):
        P.op("pe", lambda e, g=g: e.transpose(pbtb[0:bs, g * 128:(g + 1) * 128], W.BT[:, g, 0:bs], k.identb[:, :]), [W.r_BT, rc], [rpbt])
    P.op("act", lambda e: e.activation(out=W.Btok[0:bs, :], in_=pbtb[0:bs, 0:256], func=AF.Copy), [rpbt], [W.r_Btok])
    if not pre:
        chkf(k, 4.4)
    if not pre:
        pcb, rpcb = k.ps[1], k.r_ps[1]
        for g in range(2):
            P.op("pe", lambda e, g=g: e.matmul(pcb[0:bs, 256 + g * 128: 256 + g * 128 + bs], W.BT[:, g, 0:bs], W.CT[:, g, 0:bs],
                                               start=True, stop=True), [W.r_BT, W.r_CT], [rpcb])
        chkf(k, 4.5)
        for g in range(2):
            P.op("dve", lambda e, g=g: e.tensor_tensor(out=W.CBm[0:bs, g, 0:bs], in0=pcb[0:bs, 256 + g * 128: 256 + g * 128 + bs],
                                                       in1=LE[0:bs, 0:bs], op=ALU.mult), [rpcb, rc], [W.r_CBm])
        chkf(k, 5.1)
        for g in range(2):
            P.op("dve", lambda e, g=g: e.tensor_tensor(
                out=cv.R[0:bs, :, 0:bs], in0=W.dA[0:bs, g * 8:(g + 1) * 8].unsqueeze(2).to_broadcast([bs, 8, bs]),
                in1=LE[0:bs, 0:bs].unsqueeze(1).to_broadcast([bs, 8, bs]), op=ALU.mult), [W.r_dA, rc], [cv.r_R])
            psg = [k.ps[4], k.ps[5]]
            rpsg = [k.r_ps[4], k.r_ps[5]]
            hpb = max(1, 512 // bs)
            for j0 in range(0, 8, hpb):
                bk = (j0 // hpb) % 2
                P.op("pe", lambda e, j0=j0, bk=bk: e.matmul(
                    psg[bk][0:bs, 0:hpb * bs].rearrange("p (j q) -> p j q", q=bs) if False else psg[bk][0:bs, 0:min(8, hpb) * bs],
                    SU[0:bs, 0:bs], cv.R[0:bs, j0:j0 + min(8, hpb), 0:bs], start=True, stop=True), [cv.r_R, rc], [rpsg[bk]])
                nh = min(8, hpb)
                P.op("act", lambda e, j0=j0, bk=bk, nh=nh: e.activation(
                    out=cv.Lh[0:bs, j0:j0 + nh, 0:bs], in_=psg[bk][0:bs, 0:nh * bs].rearrange("p (j q) -> p j q", q=bs), func=AF.Exp),
                    [rpsg[bk]], [cv.r_Lh])
            P.op("dve", lambda e, g=g: e.tensor_tensor(
                out=W.MT[0:bs, g * 8:(g + 1) * 8, 0:bs], in0=cv.Lh[0:bs, :, 0:bs],
                in1=W.CBm[0:bs, g, 0:bs].unsqueeze(1).to_broadcast([bs, 8, bs]), op=ALU.mult), [cv.r_Lh, W.r_CBm], [W.r_MT])
        chkf(k, 5.2)
        py = [k.ps[6], k.ps[7]]
        rpy = [k.r_ps[6], k.r_ps[7]]
        for j in range(16):
            bk = j // 8
            P.op("pe", lambda e, j=j, bk=bk: e.matmul(py[bk][0:bs, (j % 8) * 64:(j % 8) * 64 + 64], W.MT[0:bs, j, 0:bs],
                                                      W.xdt[0:bs, j * 64:(j + 1) * 64], start=True, stop=True),
                 [W.r_MT, W.r_xdt], [rpy[bk]])
        chkf(k, 5.3)
        if nchb == 2:
            P.op("pool", lambda e: e.tensor_copy(out=W.CT0[:, :, 0:64], in_=W.CT[:, :, 0:64]), [W.r_CT], [W.r_CT0])
            P.op("pool", lambda e: e.tensor_copy(out=W.CT1[:, :, 64:128], in_=W.CT[:, :, 64:128]), [W.r_CT], [W.r_CT1])
    po = [k.ps[4], k.ps[5]]
    rpo = [k.r_ps[4], k.r_ps[5]]
    for cc in range(nchb):
        if not pre:
            lhs = (W.CT if nchb == 1 else (W.CT0 if cc == 0 else W.CT1))
            rl = (W.r_CT if nchb == 1 else (W.r_CT0 if cc == 0 else W.r_CT1))
            for g in range(2):
                P.op("pe", lambda e, g=g, cc=cc, lhs=lhs: e.matmul(po[g][0:bs, :], lhs[:, g, 0:bs], k.hb[cc][:, g * 512:(g + 1) * 512],
                                                                   start=(cc == 0), stop=(cc == nchb - 1)), [rl, k.r_hb[cc]], [rpo[g]])
        rows = slice(cc * 64, cc * 64 + min(64, bs))
        pst_ = [k.ps[2], k.ps[3]]
        rpst = [k.r_ps[2], k.r_ps[3]]
        for g in range(2):
            P.op("pe", lambda e, g=g, rows=rows: e.matmul(pst_[g][:, :], W.Btok[rows, g * 128:(g + 1) * 128],
                                                          W.xdtd[rows, g * 512:(g + 1) * 512], start=True, stop=True),
                 [W.r_Btok, W.r_xdtd], [rpst[g]])
        for g in range(2):
            hv = k.hs[:, g * 512:(g + 1) * 512].rearrange("p (j q) -> p j q", q=64)
            P.op("dve", lambda e, g=g, hv=hv, cc=cc: e.tensor_tensor(
                out=hv, in0=hv, in1=W.cdb[:, cc * 16 + g * 8: cc * 16 + g * 8 + 8].unsqueeze(2).to_broadcast([128, 8, 64]),
                op=ALU.mult), [k.r_hs, W.r_cdb], [k.r_hs])
            P.op("dve", lambda e, g=g: e.tensor_tensor(out=k.hs[:, g * 512:(g + 1) * 512], in0=k.hs[:, g * 512:(g + 1) * 512],
                                                       in1=pst_[g][:, :], op=ALU.add), [k.r_hs, rpst[g]], [k.r_hs])
        nxt = (cc + 1) % 2 if nchb == 2 else 0
        P.op("act", lambda e, nxt=nxt: e.activation(out=k.hb[nxt][:, :], in_=k.hs[:, :], func=AF.Copy), [k.r_hs], [k.r_hb[nxt]])
    if pre:
        return
    chkf(k, 5.4)
    P.op("pool", lambda e: e.memset(W.ss2[0:bs, :], 0.0), [], [W.r_ss2])
    for g in range(2):
        cs_ = slice(g * 512, g * 512 + 512)
        yv = W.y[0:bs, cs_].rearrange("p (j q) -> p j q", q=64)
        P.op("dve", lambda e, g=g, yv=yv: e.tensor_tensor(out=yv, in0=po[g][0:bs, :].rearrange("p (j q) -> p j q", q=64),
                                                          in1=expcum[:, g * 8:(g + 1) * 8].unsqueeze(2).to_broadcast([bs, 8, 64]),
                                                          op=ALU.mult), [rpo[g], W.r_s64], [W.r_y])
        P.op("dve", lambda e, g=g, cs_=cs_: e.tensor_tensor(out=W.y[0:bs, cs_], in0=W.y[0:bs, cs_], in1=py[g][0:bs, :], op=ALU.add),
             [W.r_y, rpy[g]], [W.r_y])
        P.op("dve", lambda e, cs_=cs_: e.tensor_tensor(out=W.y[0:bs, cs_], in0=W.y[0:bs, cs_], in1=W.xs[0:bs, cs_], op=ALU.add),
             [W.r_y, W.r_xs], [W.r_y])
        P.op("dve", lambda e, cs_=cs_: e.tensor_tensor(out=W.y[0:bs, cs_], in0=W.y[0:bs, cs_], in1=cv.sz[0:bs, b, cs_], op=ALU.mult),
             [W.r_y, cv.r_sz], [W.r_y])
        P.op("act", lambda e, g=g, cs_=cs_: e.activation(out=W.xs[0:bs, cs_], in_=W.y[0:bs, cs_], func=AF.Square,
                                                         accum_out=W.ss2[0:bs, g:g + 1]), [W.r_y, W.r_xs], [W.r_xs, W.r_ss2])
    chkf(k, 5.5)
    P.op("act", lambda e: e.activation(out=W.ss2[0:bs, :], in_=W.ss2[0:bs, :], func=AF.Sqrt, bias=EPS, scale=1.0 / 512.0),
         [W.r_ss2], [W.r_ss2])
    P.op("dve", lambda e: e.reciprocal(out=W.ss2[0:bs, :], in_=W.ss2[0:bs, :]), [W.r_ss2], [W.r_ss2])
    for g in range(2):
        cs_ = slice(g * 512, g * 512 + 512)
        P.op("dve", lambda e, g=g, cs_=cs_: e.scalar_tensor_tensor(out=W.yn[0:bs, cs_], in0=W.y[0:bs, cs_], scalar=W.ss2[0:bs, g:g + 1],
                                                                   in1=k.NG[0:bs, cs_], op0=ALU.mult, op1=ALU.mult),
             [W.r_y, W.r_ss2, rc], [W.r_yn])
    chkf(k, 5.6)
    for half in range(2):
        pt, rpt = k.ps[2 + half], k.r_ps[2 + half]
        ptb = pt[:, :].bitcast(BF16)
        for i in range(4):
            c = half * 4 + i
            P.op("pe", lambda e, ptb=ptb, c=c, i=i: e.transpose(ptb[:, i * 128:i * 128 + bs], W.yn[0:bs, c * 128:(c + 1) * 128],
                                                                k.identb[0:bs, 0:bs]), [W.r_yn, rc], [rpt])
        P.op("act", lambda e, ptb=ptb, half=half: e.activation(
            out=cv.yT[:, half * 4:half * 4 + 4, tb:tb + bs], in_=ptb[:, 0:512].rearrange("p (c t) -> p c t", c=4)[:, :, 0:bs],
            func=AF.Copy), [rpt], [cv.r_yT])


def mixer1(k, seq, Tn):
    P = k.P
    rc = k.r_const
    li = 4
    vT = k.big[:, 0:16 * TT].rearrange("p (c t) -> p c t", c=16)
    r_vT = k.r_act[0:16]
    W = k.wk
    for c in range(16):
        sb_gb, r_gb = load_slab(k, "in1", c // 2) if c % 2 == 0 else (k._gb, k._rgb)
        sb_gc, r_gc = load_slab(k, "in1", 8 + c // 2) if c % 2 == 0 else (k._gc, k._rgc)
        sb_xi, r_xi = load_slab(k, "in1", 16 + c // 2) if c % 2 == 0 else (k._xi, k._rxi)
        k._gb, k._rgb, k._gc, k._rgc, k._xi, k._rxi = sb_gb, r_gb, sb_gc, r_gc, sb_xi, r_xi
        ch = c % 2
        outs = []
        for (buf, rbuf, bk) in ((sb_gc, r_gc, 0), (sb_xi, r_xi, 1), (sb_gb, r_gb, 2)):
            pp, rpp = k.ps[bk], k.r_ps[bk]
            for kc in range(16):
                P.op("pe", lambda e, kc=kc, pp=pp, buf=buf, ch=ch: e.matmul(
                    pp[:, 0:Tn], buf[:, kc * 256 + ch * 128: kc * 256 + ch * 128 + 128], k.hT[:, kc, 0:Tn],
                    start=(kc == 0), stop=(kc == 15)), [rbuf, k.r_hT[kc]], [rpp])
            outs.append((pp, rpp))
        (pgc, rpgc), (pxi, rpxi), (pgb, rpgb) = outs
        pt, rpt_ = W.ptmp, W.r_ptmp
        P.op("act", lambda e, pgc=pgc: e.activation(out=k.sgt[0][:, 0:Tn], in_=pgc[:, 0:Tn], func=AF.Copy), [rpgc], [k.r_sgt[0]])
        P.op("pool", lambda e, c=c: e.tensor_copy(out=pt[:, 0:2], in_=k.phalo[:, c, :]), [k.r_phalo], [rpt_])
        P.op("dve", lambda e, pxi=pxi: e.tensor_tensor(out=pt[:, 2:2 + Tn], in0=k.sgt[0][:, 0:Tn], in1=pxi[:, 0:Tn], op=ALU.mult),
             [k.r_sgt[0], rpxi], [rpt_])
        P.op("pool", lambda e, c=c: e.tensor_copy(out=k.phalo[:, c, :], in_=pt[:, Tn:Tn + 2]), [rpt_], [k.r_phalo])
        u, ru = k.tmp[c % 2], k.r_tmp[c % 2]
        P.op("dve", lambda e, u=u, c=c: e.tensor_scalar(out=u[:, 0:Tn], in0=pt[:, 0:Tn], scalar1=k.swT[:, c, 0:1], scalar2=None,
                                                        op0=ALU.mult), [rpt_, rc], [ru])
        for i in (1, 2):
            P.op("dve", lambda e, u=u, c=c, i=i: e.scalar_tensor_tensor(out=u[:, 0:Tn], in0=pt[:, i:i + Tn], scalar=k.swT[:, c, i:i + 1],
                                                                        in1=u[:, 0:Tn], op0=ALU.mult, op1=ALU.add), [rpt_, ru, rc], [ru])
        P.op("dve", lambda e, u=u, c=c, pgb=pgb: e.tensor_tensor(out=vT[:, c, 0:Tn], in0=u[:, 0:Tn], in1=pgb[:, 0:Tn], op=ALU.mult),
             [ru, rpgb], [r_vT[c]])
    for s in range(8):
        buf, rbuf = load_slab(k, "out1", s)
        for ch in range(2):
            dc = 2 * s + ch
            pp, rpp = k.ps[4 + dc % 2], k.r_ps[4 + dc % 2]
            for kc in range(16):
                P.op("pe", lambda e, kc=kc, pp=pp, ch=ch, buf=buf: e.matmul(
                    pp[:, 0:Tn], buf[:, kc * 256 + ch * 128: kc * 256 + ch * 128 + 128], vT[:, kc, 0:Tn], start=(kc == 0), stop=(kc == 15)),
                    [rbuf, r_vT[kc]], [rpp])
            P.op("dve", lambda e, dc=dc, pp=pp: e.scalar_tensor_tensor(out=k.xT[:, dc, 0:Tn], in0=pp[:, 0:Tn],
                                                                       scalar=k.G[:, li, dc, seq:seq + 1], in1=k.xT[:, dc, 0:Tn],
                                                                       op0=ALU.mult, op1=ALU.add),
                 [rpp, k.r_xT[dc], rc], [k.r_xT[dc]])


def alloc_work(k):
    sb = lambda name, shape, dt: k.st.enter_context(k.nc.sbuf_tensor("s_" + name, list(shape), dt))
    W = K()
    for nm, shape, dt in (("a16", [128, 16], F32), ("dt", [128, 16], F32), ("dA", [128, 16], F32), ("s64", [128, 64], F32),
                          ("cdb", [128, 32], F32), ("xc", [128, 8, 128], F32), ("BT", [128, 2, 128], BF16), ("CT", [128, 2, 128], BF16),
                          ("CT0", [128, 2, 128], BF16), ("CT1", [128, 2, 128], BF16), ("xdt", [128, 1024], BF16),
                          ("xdtd", [128, 1024], BF16), ("xs", [128, 1024], F32), ("Btok", [128, 256], BF16),
                          ("CBm", [128, 2, 128], F32),
                          ("MT", [128, 16, 128], BF16), ("y", [128, 1024], F32), ("yn", [128, 1024], BF16), ("ss2", [128, 2], F32),
                          ("ptmp", [128, TT + 4], F32)):
        setattr(W, nm, sb("w_" + nm, shape, dt))
        setattr(W, "r_" + nm, Res("w_" + nm))
    k.wk = W
    k.PT = [sb("PT%d" % j, [64, 512], BF16) for j in range(6)]
    k.r_PT = [Res("PT%d" % j) for j in range(6)]
    P = k.P
    P.op("pool", lambda e: e.memset(full(W.CT0), 0.0), [], [W.r_CT0])
    P.op("pool", lambda e: e.memset(full(W.CT1), 0.0), [], [W.r_CT1])
    P.op("pool", lambda e: e.memset(full(carve(k).xbcT), 0.0), [], [carve(k).r_xbcT])


def tile_pass(k, kind, src, Tn, seq, dst=None, first_main=False):
    P = k.P
    P.op("sp", lambda e: e.dma_start(out=k.xT[:, :, 0:Tn], in_=src), [], k.r_xT, dma=True)
    L0 = 0 in k.layers
    L1 = 1 in k.layers
    if L0:
        norm_mod(k, 0, seq, Tn)
        ffn(k, 0, 0, seq, Tn)
        norm_mod(k, 1, seq, Tn)
        c0 = 0
        while c0 < Tn:
            tm = min(TM, Tn - c0)
            mixer0(k, seq, tm, c0, "pre" if kind == "pre" else "full", first_main and c0 == 0)
            c0 += tm
        if kind == "pre":
            return
        norm_mod(k, 2, seq, Tn)
        ffn(k, 0, 1, seq, Tn)
    if kind == "pre":
        return
    if L1:
        norm_mod(k, 3, seq, Tn)
        ffn(k, 1, 0, seq, Tn)
        norm_mod(k, 4, seq, Tn)
        mixer1(k, seq, Tn)
        if kind != "halo":
            norm_mod(k, 5, seq, Tn)
            ffn(k, 1, 1, seq, Tn)
    if kind == "halo":
        return
    norm_mod(k, 0, seq, Tn, final=True)
    P.op("pool", lambda e: e.dma_start(out=dst, in_=k.xT[:, :, 0:Tn]), k.r_xT, [], dma=True)


def scale_by_flag(k, ap, reads_writes):
    k.P.op("dve", lambda e: e.tensor_scalar(out=ap, in0=ap, scalar1=k.flag[:, 0:1], scalar2=None, op0=ALU.mult),
           list(reads_writes) + [k.r_const], list(reads_writes))


def main_phase(k):
    P, I, O = k.P, k.I, k.O
    cv = carve(k)
    alloc_work(k)
    k.st_m0 = {}
    seqh = k.seqh
    npre_tok = seqh - 128
    make_bg_jobs(k)
    k.bg_on = len(k.bg_jobs) > 0
    k.bg_slot = 0
    k.bg_slots = max(1, int(((npre_tok + TT - 1) // TT) * 38 * 0.95))
    t = 0
    while t < npre_tok:
        Tn = min(TT, npre_tok - t)
        tile_pass(k, "pre", I["xqT"][:, :, t:t + Tn], Tn, 0)
        t += Tn
    k.bg_on = False
    bg_step(k, len(k.bg_jobs))
    tile_pass(k, "halo", I["xqT"][:, :, seqh - 128:seqh], 128, 0)
    scale_by_flag(k, k.hs[:, :], [k.r_hs])
    P.op("act", lambda e: e.activation(out=k.hb[0][:, :], in_=k.hs[:, :], func=AF.Copy), [k.r_hs], [k.r_hb[0]])
    scale_by_flag(k, full(k.cvhalo), [k.r_cvhalo])
    scale_by_flag(k, full(k.phalo), [k.r_phalo])
    for ti in range(k.ntile):
        last = ti == k.ntile - 1
        k.st_m0 = {"kv_out": (O["kp"], O["vp"]), "Tn": TT} if last else {}
        tile_pass(k, "main", I["xpT"][:, :, ti * TT:(ti + 1) * TT], TT, 0, dst=O["ypT"][:, :, ti * TT:(ti + 1) * TT],
                  first_main=(ti == 0))
    P.op("pool", lambda e: e.dma_start(out=O["hp"][:, :], in_=k.hs[:, :]), [k.r_hs], [], dma=True)
    P.op("pool", lambda e: e.dma_start(out=full(O["cvp"]), in_=full(k.cvhalo)), [k.r_cvhalo], [], dma=True)
    P.op("pool", lambda e: e.dma_start(out=full(O["scp"]), in_=full(k.phalo)), [k.r_phalo], [], dma=True)
    P.op("sp", lambda e: e.dma_start(out=k.hs[:, :], in_=I["st_hT"][:, :]), [], [k.r_hs], dma=True)
    P.op("act", lambda e: e.activation(out=k.hb[0][:, :], in_=k.hs[:, :], func=AF.Copy), [k.r_hs], [k.r_hb[0]])
    P.op("sp", lambda e: e.dma_start(out=full(k.cvhalo), in_=full(I["st_cvT"])), [], [k.r_cvhalo], dma=True)
    P.op("sp", lambda e: e.dma_start(out=full(k.phalo), in_=full(I["st_scT"])), [], [k.r_phalo], dma=True)
    P.op("sp", lambda e: e.dma_start(out=k.sgt[1][:, 0:128], in_=I["ckT"][:, :]), [], [k.r_sgt[1]], dma=True)
    P.op("dve", lambda e: e.tensor_copy(out=k.KT[:, 0:128], in_=k.sgt[1][:, 0:128]), [k.r_sgt[1]], [k.r_KT])
    P.op("sp", lambda e: e.dma_start(out=k.sgt[0][0:64, 0:256].rearrange("p (c d) -> p c d", c=2), in_=full(I["cvc"])), [],
         [k.r_sgt[0]], dma=True)
    for j in range(2):
        P.op("dve", lambda e, j=j: e.tensor_copy(out=k.Va[j][:, 0:2, j * 64:(j + 1) * 64],
                                                 in_=k.sgt[0][0:64, 0:256].rearrange("p (c d) -> p c d", c=2)[:, :, j * 64:(j + 1) * 64]),
             [k.r_sgt[0]], [k.r_Va])
    k.st_m0 = {"kv_out": (O["ks"], O["vs"]), "Tn": 16}
    tile_pass(k, "sample", I["xsT"][:, :, 0:16], 16, 1, dst=O["ysT"][:, :, 0:16])
    P.op("pool", lambda e: e.dma_start(out=O["hs"][:, :], in_=k.hs[:, :]), [k.r_hs], [], dma=True)
    P.op("pool", lambda e: e.dma_start(out=full(O["cvs"]), in_=full(k.cvhalo)), [k.r_cvhalo], [], dma=True)
    P.op("pool", lambda e: e.dma_start(out=full(O["scs"]), in_=full(k.phalo)), [k.r_phalo], [], dma=True)


def _slabs_k16(W, ncol_slab=256):
    Kd, N = W.shape
    ns = N // ncol_slab
    return np.ascontiguousarray(W.reshape(Kd // 128, 128, ns, ncol_slab).transpose(2, 1, 0, 3)).reshape(ns, 128, -1)


def _slabs_down(Wd):
    a = Wd.reshape(2, 22, 128, 16, 128).transpose(3, 0, 2, 1, 4)
    return np.ascontiguousarray(a).reshape(32, 128, 22 * 128)


def _fm(x):
    T, Dd = x.shape
    return np.ascontiguousarray(x.reshape(T, Dd // 128, 128).transpose(2, 1, 0))


def _unfm(a):
    return np.ascontiguousarray(a.transpose(2, 1, 0)).reshape(a.shape[2], -1)


def _consts():
    f = np.float32
    r = np.arange(128)
    same = (r[:, None] // 64) == (r[None, :] // 64)
    LE = (same & (r[:, None] <= r[None, :])).astype(f)
    SU = (same & (r[:, None] > r[None, :])).astype(f)
    BLK = same.astype(f)
    SEL0 = np.repeat((r < 64).astype(f)[:, None], 128, axis=1)
    SEL1 = np.repeat((r >= 64).astype(f)[:, None], 128, axis=1)
    cmask = np.ascontiguousarray(np.stack([LE, SU, BLK, SEL0, SEL1], axis=1))
    slope0 = (2.0 ** (-8.0 * np.arange(1, 9) / 16.0)).astype(f)
    kk = np.arange(64)[:, None, None]
    qq = np.arange(64)[None, None, :]
    al = np.zeros((64, 3, 8, 64), f)
    for m in range(3):
        dist = np.abs(qq + 64 * m - kk).astype(f)
        al[:, m] = -(slope0[None, :, None] * dist)
    return cmask, np.ascontiguousarray(al.reshape(64, 3 * 512))


def prep_shared(inp, layers=(0, 1), ffn_on=True):
    sh = {}
    f = np.float32
    sh["ident"] = np.eye(128, dtype=f)
    sh["cmask"], sh["alibi"] = _consts()
    sh["norm_gT"] = np.ascontiguousarray(inp["norm_g"].reshape(6, 16, 128).transpose(2, 0, 1))
    sh["fnormT"] = np.ascontiguousarray(inp["final_norm_g"].reshape(16, 128).T)
    sh["b_adaT"] = np.ascontiguousarray(inp["b_ada"].reshape(288, 128).T)
    sh["w_ada"] = np.concatenate([_slabs_k16(inp["w_ada"][l]) for l in range(2)], axis=0)
    rep = lambda v: np.ascontiguousarray(np.broadcast_to(np.asarray(v, f).reshape(1, -1), (128, v.size)))
    sk = inp["attn_sinks"][0]
    sh["sinks2"] = np.ascontiguousarray(np.concatenate([np.broadcast_to(sk[0:8], (64, 8)), np.broadcast_to(sk[8:16], (64, 8))], axis=0))
    sh["dtb_b"] = rep(inp["ssd_dt_bias"][0])
    sh["alog_b"] = rep(inp["ssd_a_log"][0])
    sh["d_b"] = rep(inp["ssd_d"][0])
    sh["ng_b"] = rep(inp["ssd_norm_g"][0])
    sh["conv_wT"] = np.ascontiguousarray(inp["ssd_conv_w"][0].reshape(4, 12, 128).transpose(2, 1, 0))
    sh["conv_bT"] = np.ascontiguousarray(inp["ssd_conv_b"][0].reshape(12, 128).T)
    sh["sconv_wT"] = np.ascontiguousarray(inp["sconv_w"][0].reshape(3, 16, 128).transpose(2, 1, 0))
    W = {}
    for l in range(2):
        for i in range(2):
            if ffn_on and l in layers:
                W["gu%d%d" % (l, i)] = np.concatenate([_slabs_k16(inp["w_ffn_gate"][l, i], 128),
                                                       _slabs_k16(inp["w_ffn_up"][l, i], 128)], axis=0)
                W["dn%d%d" % (l, i)] = _slabs_down(inp["w_ffn_down"][l, i])
    if 0 in layers:
        w0 = inp["w_in_mix0"][0]
        wq = w0[:, 0:1024].reshape(2048, 2, 8, 64).transpose(0, 2, 1, 3).reshape(2048, 1024)
        w0p = np.zeros((2048, 4096), f)
        w0p[:, 0:1024] = wq
        w0p[:, 1024:3856] = w0[:, 1024:3856]
        W["in0"] = _slabs_k16(w0p)
        wo = inp["w_out_mix0"][0]
        woa = wo[0:1024].reshape(2, 8, 64, 2048).transpose(1, 0, 2, 3).reshape(1024, 2048)
        W["out0"] = _slabs_k16(np.concatenate([woa, wo[1024:2048]], axis=0))
    if 1 in layers:
        W["in1"] = _slabs_k16(inp["w_in_mix1"][0])
        W["out1"] = _slabs_k16(inp["w_out_mix1"][0])
    for name, a in W.items():
        sh["w_" + name] = a
    return sh


def prep_core(inp, i, sh, ntile=NTILE):
    f = np.float32
    s, half = i // 2, i % 2
    seqh = ntile * TT
    m = dict(sh)
    m["xpT"] = _fm(inp["x_prompt"][s, half * seqh:(half + 1) * seqh])
    m["xqT"] = _fm(inp["x_prompt"][s, 0:seqh]) if half == 1 else np.zeros((128, 16, seqh), f)
    m["xsT"] = _fm(inp["x_sample"][i])
    m["c2T"] = _fm(np.stack([inp["c_prompt"][s], inp["c_sample"][i]], axis=0))
    m["flag"] = np.full((128, 1), float(half), f)
    m["ckT"] = np.ascontiguousarray(inp["cache_swa_k"][0, i].reshape(128, 128).T)
    m["cvc"] = np.ascontiguousarray(inp["cache_swa_v"][0, i].reshape(2, 64, 128).transpose(1, 0, 2))
    m["st_hT"] = np.ascontiguousarray(inp["state_ssd"][0, i].reshape(1024, 128).T)
    m["st_cvT"] = _fm(inp["state_ssd_conv"][0, i])
    m["st_scT"] = _fm(inp["state_sconv"][0, i])
    return m


def assemble(results, ntile=NTILE, cores=None):
    f = np.float32
    seqh = ntile * TT
    cores = list(range(NCORES)) if cores is None else cores
    yp = np.zeros((4, 2 * seqh, D), f)
    ys = np.zeros((8, 16, D), f)
    kp = np.zeros((1, 4, 128, 2, 64), f)
    vp = np.zeros((1, 4, 128, 2, 64), f)
    hp = np.zeros((1, 4, 16, 64, 128), f)
    cvp = np.zeros((1, 4, 3, 1536), f)
    scp = np.zeros((1, 4, 2, 2048), f)
    ks = np.zeros((1, 8, 16, 2, 64), f)
    vs = np.zeros((1, 8, 16, 2, 64), f)
    hs = np.zeros((1, 8, 16, 64, 128), f)
    cvs = np.zeros((1, 8, 3, 1536), f)
    scs = np.zeros((1, 8, 2, 2048), f)
    for idx, i in enumerate(cores):
        r = results[idx]
        s, half = i // 2, i % 2
        yp[s, half * seqh:(half + 1) * seqh] = _unfm(r["ypT"])
        ys[i] = _unfm(r["ysT"])
        if half == 1:
            kp[0, s] = r["o_kp"].reshape(128, 2, 64)
            vp[0, s] = r["o_vp"].reshape(128, 2, 64)
            hp[0, s] = r["o_hp"].T.reshape(16, 64, 128)
            cvp[0, s] = _unfm(r["o_cvp"])
            scp[0, s] = _unfm(r["o_scp"])
        ks[0, i] = r["o_ks"].reshape(16, 2, 64)
        vs[0, i] = r["o_vs"].reshape(16, 2, 64)
        hs[0, i] = r["o_hs"].T.reshape(16, 64, 128)
        cvs[0, i] = _unfm(r["o_cvs"])
        scs[0, i] = _unfm(r["o_scs"])
    return (yp, ys, kp, vp, hp, cvp, scp, ks, vs, hs, cvs, scs)


_NC_CACHE = {}


def kernel(**inputs):
    inp = {k_: np.asarray(v) for k_, v in inputs.items()}
    if "nc" not in _NC_CACHE:
        _NC_CACHE["nc"] = build()
    nc = _NC_CACHE["nc"]
    sh = prep_shared(inp)
    in_maps = [prep_core(inp, i, sh) for i in range(NCORES)]
    res = run_bass_kernel_spmd(nc, in_maps, core_ids=list(range(NCORES)))
    return assemble(res.results)
```
